# Optimizing a Trainium2 kernel written in Bass

```python
import jax, jax.numpy as jnp
from jax import lax
import numpy as np

D_MODEL = 1024
BATCH = 2
SEQ = 8192
DEPTH = 2

CHUNK = 128
A_GROUPS = 8
A_WIDTH = 512
A_HEAD = A_WIDTH // A_GROUPS
B_WIDTH = 512
CONV_WIDTH = 3
C_WIDTH = 512
POOL_WINDOWS = (2, 4, 8, 16)
C_GROUP = C_WIDTH // len(POOL_WINDOWS)
IN_TOTAL = 3 * A_WIDTH + 4 * B_WIDTH + 2 * C_WIDTH + 3 * D_MODEL
RMS_EPS = 1e-6
LN_EPS = 1e-5

kernel_name = "hybrid_gmlp_shortconv_pool_gated_merge"


def _rmsnorm(x, g):
    xf = x.astype(jnp.float32)
    y = xf * lax.rsqrt(jnp.mean(xf * xf, axis=-1, keepdims=True) + RMS_EPS)
    return (y * g.astype(jnp.float32)).astype(x.dtype)


def _layernorm(x, g, b):
    xf = x.astype(jnp.float32)
    mu = jnp.mean(xf, axis=-1, keepdims=True)
    xc = xf - mu
    var = jnp.mean(xc * xc, axis=-1, keepdims=True)
    y = xc * lax.rsqrt(var + LN_EPS)
    return (y * g.astype(jnp.float32) + b.astype(jnp.float32)).astype(x.dtype)


def _split_points():
    widths = [A_WIDTH] * 3 + [B_WIDTH] * 4 + [C_WIDTH] * 2 + [D_MODEL] * 3
    return [int(s) for s in np.cumsum(widths)[:-1]]


def _gmlp_branch(u, v, ln_g, ln_b, w_s, b_s):
    u = jax.nn.gelu(u)
    v = _layernorm(jax.nn.gelu(v), ln_g, ln_b)
    bsz, s, _ = v.shape
    vc = v.reshape(bsz, s // CHUNK, CHUNK, A_GROUPS, A_HEAD)
    causal = jnp.tril(jnp.ones((CHUNK, CHUNK), dtype=bool))
    w_m = jnp.where(causal, w_s, 0.0)
    sg = jnp.einsum('gts,bnsgc->bntgc', w_m, vc) + b_s.T[:, :, None]
    return u * sg.reshape(bsz, s, A_WIDTH)


def _shortconv_branch(xb, bg, cg, conv_w, conv_b):
    y = cg * xb
    y = lax.conv_general_dilated(
        y, conv_w[:, None, :].astype(y.dtype), window_strides=(1,),
        padding=[(CONV_WIDTH - 1, 0)], dimension_numbers=('NWC', 'WIO', 'NWC'),
        feature_group_count=B_WIDTH) + conv_b
    return bg * y


def _pool_branch(xc, w_pool, pool_scale):
    bsz, s, _ = xc.shape
    xf = xc.astype(jnp.float32).reshape(bsz, s, len(POOL_WINDOWS), C_GROUP)
    cs = jnp.cumsum(xf, axis=1)
    t_count = jnp.arange(1, s + 1, dtype=jnp.float32)
    pooled = []
    for gi, w in enumerate(POOL_WINDOWS):
        c = cs[:, :, gi]
        lag = jnp.pad(c[:, :s - w], ((0, 0), (w, 0), (0, 0)))
        cnt = jnp.minimum(t_count, float(w))[None, :, None]
        pooled.append((c - lag) / cnt)
    pooled = (jnp.stack(pooled, axis=2) - xf).astype(xc.dtype)
    y = jnp.einsum('bsgc,gcd->bsgd', pooled, w_pool).reshape(bsz, s, C_WIDTH)
    return y * pool_scale


def _hybrid_layer(x, norm_g, w_in, ln_g, ln_b, w_s, b_s, conv_w, conv_b,
                  w_pool, pool_scale, w_pa, w_pb, w_pc, w_o):
    h = _rmsnorm(x, norm_g)
    p = h @ w_in
    (u, v, z_a, x_b, b_g, c_g, z_b, x_c, z_c,
     g_a, g_b, g_c) = jnp.split(p, _split_points(), axis=-1)
    y_a = (_gmlp_branch(u, v, ln_g, ln_b, w_s, b_s) * jax.nn.silu(z_a)) @ w_pa
    y_b = (_shortconv_branch(x_b, b_g, c_g, conv_w, conv_b) * jax.nn.silu(z_b)) @ w_pb
    y_c = (_pool_branch(x_c, w_pool, pool_scale) * jax.nn.silu(z_c)) @ w_pc
    merged = (jax.nn.sigmoid(g_a) * y_a + jax.nn.sigmoid(g_b) * y_b
              + jax.nn.sigmoid(g_c) * y_c)
    return x + merged @ w_o


def setup_inputs(seed: int = 0) -> dict:
    key = jax.random.key(seed)
    ks = jax.random.split(key, 20)
    f32 = jnp.float32
    n = lambda k, shape: jax.random.normal(k, shape, dtype=f32)
    return {
        "x": n(ks[0], (BATCH, SEQ, D_MODEL)),
        "norm_g": 1.0 + 0.02 * n(ks[1], (DEPTH, D_MODEL)),
        "w_in": n(ks[2], (DEPTH, D_MODEL, IN_TOTAL)) * D_MODEL ** -0.5,
        "ln_g": 1.0 + 0.02 * n(ks[3], (DEPTH, A_WIDTH)),
        "ln_b": 0.02 * n(ks[4], (DEPTH, A_WIDTH)),
        "w_s": n(ks[5], (DEPTH, A_GROUPS, CHUNK, CHUNK)) * CHUNK ** -0.5,
        "b_s": 1.0 + 0.1 * n(ks[6], (DEPTH, A_GROUPS, CHUNK)),
        "conv_w": n(ks[7], (DEPTH, CONV_WIDTH, B_WIDTH)) * CONV_WIDTH ** -0.5,
        "conv_b": 0.02 * n(ks[8], (DEPTH, B_WIDTH)),
        "w_pool": n(ks[9], (DEPTH, len(POOL_WINDOWS), C_GROUP, C_GROUP)) * C_GROUP ** -0.5,
        "pool_scale": 1.0 + 0.02 * n(ks[10], (DEPTH, C_WIDTH)),
        "w_pa": n(ks[11], (DEPTH, A_WIDTH, D_MODEL)) * A_WIDTH ** -0.5,
        "w_pb": n(ks[12], (DEPTH, B_WIDTH, D_MODEL)) * B_WIDTH ** -0.5,
        "w_pc": n(ks[13], (DEPTH, C_WIDTH, D_MODEL)) * C_WIDTH ** -0.5,
        "w_o": n(ks[14], (DEPTH, D_MODEL, D_MODEL)) * D_MODEL ** -0.5,
        "final_g": 1.0 + 0.02 * n(ks[15], (D_MODEL,)),
    }


def reference(x, norm_g, w_in, ln_g, ln_b, w_s, b_s, conv_w, conv_b, w_pool,
              pool_scale, w_pa, w_pb, w_pc, w_o, final_g):
    for l in range(DEPTH):
        x = _hybrid_layer(x, norm_g[l], w_in[l], ln_g[l], ln_b[l], w_s[l], b_s[l],
                          conv_w[l], conv_b[l], w_pool[l], pool_scale[l],
                          w_pa[l], w_pb[l], w_pc[l], w_o[l])
    return _rmsnorm(x, final_g)
```

```python
import numpy as np
from contextlib import ExitStack
import concourse.bass as bass
import concourse.mybir as mybir
from concourse.bass_utils import run_bass_kernel_spmd

F32 = mybir.dt.float32
F32R = mybir.dt.float32r
AF = mybir.ActivationFunctionType
ALU = mybir.AluOpType

D = 1024
NCORES = 8
TOK_CORE = 2048
NTILES = 17
SUPER = [6, 6, 5]
NTM = 6
TM = NTM * 128
DEPTH = 2
HALO_SKIP = True
_DBG = False
SLOT = 2048
NSLOT_L = 40
RING = 6
RMS_EPS = 1e-6
LN_EPS = 1e-5

C_U, C_V, C_ZA, C_XB, C_BG, C_CG, C_ZB, C_XC, C_ZC, C_GA, C_GB, C_GC = (
    0, 512, 1024, 1536, 2048, 2560, 3072, 3584, 4096, 4608, 5632, 6656)

CL_G, CL_LNB, CL_CW, CL_CB, CL_PS = 0, 8, 12, 24, 28
CL_LNG, CL_BS, CL_WS = 32, 36, 548
CL_LNGB = 1572
CL = 2084
CR_L = 512
CGW = 320


def _mg(*ds):
    r = {}
    for d in ds:
        if not d:
            continue
        for k, v in d.items():
            if r.get(k, 0) < v:
                r[k] = v
    return r


class Sched:
    def __init__(self):
        self.ops = {}
        self.cnt = {}

    def op(self, eng, fn, deps=None, sem=None, amt=1, ms=True):
        tok = None
        semname = None
        if ms:
            semname = sem or eng
            self.cnt[semname] = self.cnt.get(semname, 0) + amt
            tok = {semname: self.cnt[semname]}
        self.ops.setdefault(eng, []).append((deps or {}, fn, semname, amt))
        return tok

    def run(self, eng_name, eng, sems):
        known = {}
        for deps, fn, semname, amt in self.ops.get(eng_name, []):
            for s, v in deps.items():
                if known.get(s, 0) >= v:
                    continue
                eng.wait_ge(sems[s], v)
                known[s] = v
            ins = fn(eng)
            if semname is not None:
                ins.then_inc(sems[semname], amt)


class Buf:
    def __init__(self):
        self.w = {}
        self.r = {}

    def rd(self):
        return self.w

    def wr(self):
        return _mg(self.w, self.r)

    def wrote(self, tok, fresh=True):
        if fresh:
            self.w = dict(tok)
            self.r = {}
        else:
            self.w = _mg(self.w, tok)

    def read(self, tok):
        self.r = _mg(self.r, tok)


def build_program():
    nc = bass.Bass("TRN2", target_bir_lowering=False)
    nc.dge_precook = False
    x_d = nc.dram_tensor("x", [NTILES * 128, D], F32, kind="ExternalInput").ap()
    ws_d = nc.dram_tensor("ws", [DEPTH * NSLOT_L, 128, SLOT], F32, kind="ExternalInput").ap()
    cl_d = nc.dram_tensor("cl", [DEPTH, 128, CL], F32, kind="ExternalInput").ap()
    cr_d = nc.dram_tensor("cr", [128, DEPTH * CR_L + 1664], F32, kind="ExternalInput").ap()
    cg_d = nc.dram_tensor("cg", [128, CGW], F32, kind="ExternalInput").ap()
    fg_d = nc.dram_tensor("fg", [128, D], F32, kind="ExternalInput").ap()
    out_d = nc.dram_tensor("out", [TOK_CORE, D], F32, kind="ExternalOutput").ap()
    x1_d = nc.dram_tensor("x1", [NTILES * 128, D], F32).ap()
    dbg_d = nc.dram_tensor("dbg", [8, 128, 6144], F32, kind="ExternalOutput").ap() if _DBG else None

    S = Sched()
    es = ExitStack()

    def sb(name, shape, dt):
        return es.enter_context(nc.sbuf_tensor(name, shape, dt))

    with es:
        hT = sb("hT", [128, 8, TM], F32R)
        mT = sb("mT", [128, 8, TM], F32R)
        yT = sb("yT", [128, 4, TM], F32R)
        ring = sb("ring", [128, RING, SLOT], F32R)
        scr = sb("scr", [128, 7168], F32)
        scrR = sb("scrR", [128, 3584], F32R)
        xt = sb("xt", [128, 2, D], F32)
        xs = sb("xs", [128, 2, D], F32)
        sig = sb("sig", [128, 2, TM], F32)
        clt = sb("clt", [128, CL], F32)
        crt = sb("crt", [128, DEPTH * CR_L + 1664], F32R)
        cgt = sb("cgt", [128, CGW], F32)
        wm = sb("wm", [128, 1024], F32R)
        bias2 = sb("bias2", [128, 4, 128], F32)
        ssq = sb("ssq", [128, 8], F32)
        rst = sb("rst", [128, 8], F32)
        tmp8 = sb("tmp8", [128, 8], F32)
        bnst = sb("bnst", [128, NTM, 6], F32)
        mv = sb("mv", [128, NTM, 2], F32)
        rl = sb("rl", [128, NTM], F32)
        rl2 = sb("rl2", [128, NTM], F32)
        rl3 = sb("rl3", [128, NTM], F32)
        nmr = sb("nmr", [128, NTM], F32)
        s1t = sb("s1t", [128, NTM], F32)
        s2t = sb("s2t", [128, NTM], F32)
        mean_t = sb("mean_t", [128, NTM], F32)
        m2t = sb("m2t", [128, NTM], F32)
        var_t = sb("var_t", [128, NTM], F32)
        tmp9 = sb("tmp9", [128, 8], F32)
        yyc = sb("yyc", [128, 4, 2], F32)
        xcc = sb("xcc", [128, 4, 16], F32)
        ps = es.enter_context(nc.psum_tensor("ps", [128, 8, 512], F32))

        dbg_toks = {}
        sem_names = ["dbgs", "pe", "act", "dve", "pool", "cg", "cr", "cl", "xl0", "xl1", "xin0", "xin1", "sx0", "sx1", "sx2", "sx3", "sx4", "sx5", "so0", "so1", "so2", "so3", "so4", "so5", "sto", "fgl"] + \
            ["w%d" % s for s in range(RING)]
        sems = {n: es.enter_context(nc.semaphore(n)) for n in sem_names}

        def mm(out, lhsT, rhs, start, stop, deps=None, ms=False):
            return S.op("pe", lambda e: e.matmul(out, lhsT, rhs, start=start, stop=stop), deps, ms=ms)

        def tr(out, in_, ident, deps=None, ms=False):
            return S.op("pe", lambda e: e.transpose(out, in_, ident), deps, ms=ms)

        def act(out, in_, func, deps=None, scale=None, accum=None):
            def f(e):
                kw = {}
                if scale is not None:
                    kw["scale"] = scale
                if accum is not None:
                    kw["accum_out"] = accum
                return e.activation(out=out, in_=in_, func=func, **kw)
            return S.op("act", f, deps)

        def ts(out, in0, s1, s2, op0, op1=None, deps=None):
            def f(e):
                if op1 is None:
                    return e.tensor_scalar(out=out, in0=in0, scalar1=s1, scalar2=None, op0=op0)
                return e.tensor_scalar(out=out, in0=in0, scalar1=s1, scalar2=s2, op0=op0, op1=op1)
            return S.op("dve", f, deps)

        def tt(out, in0, in1, op, deps=None):
            return S.op("dve", lambda e: e.tensor_tensor(out=out, in0=in0, in1=in1, op=op), deps)

        def stt(out, in0, sc, in1, op0, op1, deps=None):
            return S.op("dve", lambda e: e.scalar_tensor_tensor(out=out, in0=in0, scalar=sc, in1=in1, op0=op0, op1=op1), deps)

        def cp(out, in_, deps=None):
            return S.op("dve", lambda e: e.tensor_copy(out=out, in_=in_), deps)

        def rsqrt(dst, src, ta, tb, scale, eps, deps):
            t = ts(ta, src, scale, eps, ALU.mult, ALU.add, deps)
            t = act(tb, ta, AF.Sqrt, t)
            return S.op("dve", lambda e: e.reciprocal(out=dst, in_=tb), t)

        def dma(q, out, in_, sem, deps=None):
            return S.op(q, lambda e: e.dma_start(out=out, in_=in_), deps, sem=sem, amt=16)

        def v3(ap, b):
            return ap.rearrange("p (a b) -> p a b", b=b)

        def dump(idx, ap, width, deps, bufs=()):
            if not _DBG:
                return
            t = dma("pool", dbg_d[idx, :, 0:width], ap, "dbgs", deps)
            for b_ in bufs:
                b_.read(t)
            scr_touch(t)
            dbg_toks.update(_mg(dbg_toks, t))

        ps_rd = [dict() for _ in range(4)]
        ps_nxt = [0]

        def ps_alloc():
            p = ps_nxt[0]
            ps_nxt[0] = (p + 1) % 4
            d = ps_rd[p]
            ps_rd[p] = {}
            return p, d

        def ps_rel(p, tok):
            ps_rd[p] = _mg(ps_rd[p], tok)

        total_items = DEPTH * len(SUPER) * NSLOT_L
        rg = {"q": 0, "c": 0}
        slot_rd = [dict() for _ in range(RING)]
        slot_tok = [None] * RING

        slot_free = [True] * RING

        def ring_issue():
            while rg["q"] < total_items and slot_free[rg["q"] % RING]:
                q = rg["q"]
                s = q % RING
                layer = q // (len(SUPER) * NSLOT_L)
                item = q % NSLOT_L
                src = ws_d[layer * NSLOT_L + item].bitcast(F32R)
                slot_tok[s] = dma("sp", ring[:, s, :], src, "w%d" % s, slot_rd[s])
                slot_rd[s] = {}
                slot_free[s] = False
                rg["q"] = q + 1

        def ring_next():
            c = rg["c"]
            rg["c"] = c + 1
            s = c % RING
            return s, slot_tok[s]

        def ring_rel(s, tok):
            slot_rd[s] = _mg(slot_rd[s], tok)
            slot_free[s] = True
            ring_issue()

        cst_g = _mg(dma("pool", cgt[:], cg_d, "cg"), dma("pool", crt[:], cr_d.bitcast(F32R), "cr"))
        ident = cgt[:, 0:128]
        maskT = cgt[:, 128:256]
        invc = v3(cgt[:, 256:320], 16)
        GB = DEPTH * CR_L
        band = crt[:, GB:GB + 512]
        bandp = crt[:, GB + 512:GB + 1024]
        bandf = crt[:, GB + 1024:GB + 1536]
        ones = crt[:, GB + 1536:GB + 1664]

        ring_issue()

        clB = Buf()
        wmB = Buf()
        hTB = Buf()
        mTB = [Buf() for _ in range(8)]
        yTB = Buf()
        xtB = [Buf(), Buf()]
        xsB = [Buf(), Buf()]
        sigB = [Buf(), Buf()]
        statB = [Buf() for _ in range(8)]
        yycB = Buf()
        xccB = Buf()
        scr_prev = [dict()]
        scr_cur = [dict()]
        x1_st = {}
        out_toks = {}

        def scr_touch(tok):
            scr_cur[0] = _mg(scr_cur[0], tok)
            return tok

        def scr_phase():
            scr_prev[0] = _mg(scr_prev[0], scr_cur[0])
            scr_cur[0] = {}

        xl_cnt = [0]


        ssq2 = sb("ssq2", [128, 8], F32)
        rst2 = sb("rst2", [128, 8], F32)
        tmpa2 = sb("tmpa2", [128, 8], F32)
        tmpb2 = sb("tmpb2", [128, 8], F32)
        stat2B = [Buf() for _ in range(8)]
        junkB = Buf()
        stA_B = Buf()
        g_fm = clt[:, CL_G:CL_G + 8]
        lnb = clt[:, CL_LNB:CL_LNB + 4]
        cw = v3(clt[:, CL_CW:CL_CW + 12], 3)
        cb = clt[:, CL_CB:CL_CB + 4]
        psc = clt[:, CL_PS:CL_PS + 4]
        lngf = clt[:, CL_LNG:CL_LNG + 4]
        lng = clt[:, CL_LNGB:CL_LNGB + 512]
        bsb = v3(clt[:, CL_BS:CL_BS + 512], 128)
        wsT = clt[:, CL_WS:CL_WS + 1024]
        ctx = {}

        def all_now():
            return {e_: S.cnt[e_] for e_ in ("pe", "act", "dve") if S.cnt.get(e_)}

        def setup_A(l, bar=None):
            bar = bar or {}
            t = dma("pool", clt[:], cl_d[l], "cl", bar)
            cld = _mg(t, cst_g)
            ctx["cld"] = cld
            ctx["bar"] = bar
            wtoks = {}
            for g in range(8):
                tk = tt(wm[:, g * 128:(g + 1) * 128], wsT[:, g * 128:(g + 1) * 128], maskT, ALU.mult, _mg(cld, bar))
                wtoks = _mg(wtoks, tk)
            ctx["wtoks"] = wtoks
            t = S.op("dve", lambda e: e.memset(yyc[:], 0.0), bar)
            yycB.wrote(t)
            t = S.op("dve", lambda e: e.memset(xcc[:], 0.0), bar)
            xccB.wrote(t)

        def setup_B(l):
            cld = ctx["cld"]
            bar = ctx["bar"]
            wtoks = ctx["wtoks"]
            p, pd = ps_alloc()
            tk = None
            for g in range(8):
                tk = mm(ps[:, 2 * p + g // 4, (g % 4) * 128:(g % 4 + 1) * 128], ones, wm[:, g * 128:(g + 1) * 128],
                        True, True, _mg(pd, wtoks), ms=True)
            btok = {}
            for fc in range(4):
                for hf in range(2):
                    g = 2 * fc + hf
                    sl = slice(hf * 64, hf * 64 + 64)
                    t2 = stt(bias2[sl, fc, :], ps[sl, 2 * p + g // 4, (g % 4) * 128:(g % 4 + 1) * 128],
                             lnb[sl, fc:fc + 1], bsb[sl, fc, :], ALU.mult, ALU.add, _mg(tk, cld, bar))
                    btok = _mg(btok, t2)
            ps_rel(p, btok)
            ctx["wmd"] = _mg(wtoks, btok)

        def st_of(gt):
            return [k for k in range(len(SUPER)) if sum(SUPER[:k]) <= gt < sum(SUPER[:k + 1])][0]

        def phase0_A(l, gt, i):
            xsrc = x_d if l == 0 else x1_d
            b = xl_cnt[0] % 2
            xl_cnt[0] += 1
            ld_dep = xtB[b].wr()
            if l > 0:
                ld_dep = _mg(ld_dep, x1_st[(l - 1, st_of(gt))])
            t = dma("pool", xt[:, b, :], xsrc[gt * 128:(gt + 1) * 128, :], "xl%d" % b, ld_dep)
            xtB[b].wrote(t)
            t = act(xs[:, b, :], xt[:, b, :], AF.Square, _mg(xtB[b].rd(), xsB[b].wr(), statB[i].wr()), accum=ssq[:, i:i + 1])
            xtB[b].read(t)
            t = rsqrt(rst[:, i:i + 1], ssq[:, i:i + 1], tmp8[:, i:i + 1], tmp9[:, i:i + 1], 1.0 / D, RMS_EPS, t)
            t = act(xs[:, b, :], xt[:, b, :], AF.Copy, t, scale=rst[:, i:i + 1])
            statB[i].wrote(t)
            xtB[b].read(t)
            xsB[b].wrote(t)
            return (b, i)

        def phase0_B(st, hdeps):
            b, i = st
            cld = ctx["cld"]
            p, pd = ps_alloc()
            tk = None
            for k in range(8):
                tk = tr(ps[:, 2 * p + k // 4, (k % 4) * 128:(k % 4 + 1) * 128], xs[:, b, k * 128:(k + 1) * 128], ident,
                        _mg(pd, xsB[b].rd(), cst_g) if k == 0 else None, ms=(k == 7))
            xsB[b].read(tk)
            et = {}
            for k0 in (0, 4):
                t2 = tt(hT[:, k0:k0 + 4, i * 128:(i + 1) * 128], v3(ps[:, 2 * p + k0 // 4, :], 128),
                        g_fm[:, k0:k0 + 4].rearrange("p (k o) -> p k o", o=1).to_broadcast([128, 4, 128]),
                        ALU.mult, _mg(tk, hdeps, cld))
                et = _mg(et, t2)
            ps_rel(p, et)
            ctx.setdefault("hT_tile", {})[i] = _mg(et, hdeps)
            return et

        passes = [(l, j) for l in range(DEPTH) for j in range(len(SUPER))]
        setup_A(0)
        setup_B(0)
        hw = {}
        hdeps0 = hTB.wr()
        st_ = phase0_A(0, 0, 0)
        for i in range(SUPER[0]):
            nst_ = phase0_A(0, i + 1, i + 1) if i + 1 < SUPER[0] else None
            hw = _mg(hw, phase0_B(st_, hdeps0))
            st_ = nst_
        hTB.wrote(hw)

        for n, (l, j) in enumerate(passes):
            if True:
                NT = SUPER[j]
                t0 = sum(SUPER[:j])
                xsrc = x_d if l == 0 else x1_d
                cld = ctx["cld"]
                wmd = ctx["wmd"]
                wpool = crt[:, l * CR_L:(l + 1) * CR_L]
                sk = 1 if (l > 0 and l == DEPTH - 1 and j == 0 and HALO_SKIP) else 0
                co = sk * 128
                TX = NT * 128
                T = (NT - sk) * 128
                cwid = T // 2
                hd = hTB.rd()
                if l == 0 and j == 0:
                    dump(0, hT[:].bitcast(F32).rearrange("p a b -> p (a b)"), 6144, hd, [hTB])

                def fm_job(s, off, kch, rhs3, rdeps, coff=0, Tc=None):
                    Tc = Tc or T
                    cw_ = Tc // 2
                    p, pd = ps_alloc()
                    tk = None
                    for ci, (c0, c1) in enumerate([(0, cw_), (cw_, Tc)]):
                        for k in range(kch):
                            tk = mm(ps[:, 2 * p + ci, 0:cw_], ring[:, s, off + k * 128:off + (k + 1) * 128],
                                    rhs3[:, k, coff + c0:coff + c1], k == 0, k == kch - 1,
                                    _mg(pd, rdeps, slot_tok[s]) if (k == 0) else None, ms=(k == kch - 1))
                    return p, tk

                def psv(p, Tc=None):
                    return ps[:, 2 * p:2 * p + 2, 0:(Tc or T) // 2]

                def bv(ap2, Tc=None):
                    return v3(ap2, (Tc or T) // 2)

                def outproj(br, first):
                    yd = yTB.rd()
                    order = ["g0", "wp0", "g1", "g2", "wp1", "g3"]
                    it = iter(order)
                    held = {}
                    used = {}

                    def get(name):
                        while name not in held:
                            nm = next(it)
                            s, _ = ring_next()
                            held[nm] = s
                            used[nm] = {}
                        return held[name]

                    for oc in range(8):
                        gs = get("g%d" % (oc // 2))
                        ws_ = get("wp%d" % (oc // 4))
                        p, tk = fm_job(gs, (oc % 2) * 1024, 8, hT, hd, coff=co)
                        used["g%d" % (oc // 2)] = _mg(used["g%d" % (oc // 2)], tk)
                        hTB.read(tk)
                        sb_ = oc % 2
                        t = act(bv(sig[:, sb_, 0:T]), psv(p), AF.Sigmoid, _mg(tk, sigB[sb_].wr()))
                        ps_rel(p, t)
                        sigB[sb_].wrote(t)
                        p, tk = fm_job(ws_, (oc % 4) * 512, 4, yT, yd)
                        used["wp%d" % (oc // 4)] = _mg(used["wp%d" % (oc // 4)], tk)
                        yTB.read(tk)
                        if first:
                            t = tt(bv(mT[:, oc, 0:T]), psv(p), bv(sig[:, sb_, 0:T]), ALU.mult,
                                   _mg(tk, sigB[sb_].rd(), mTB[oc].wr()))
                            ps_rel(p, t)
                            sigB[sb_].read(t)
                            mTB[oc].wrote(t)
                        else:
                            t = tt(bv(sig[:, sb_, 0:T]), psv(p), bv(sig[:, sb_, 0:T]), ALU.mult,
                                   _mg(tk, sigB[sb_].rd()))
                            ps_rel(p, t)
                            t = tt(mT[:, oc, 0:T], mT[:, oc, 0:T].bitcast(F32), sig[:, sb_, 0:T], ALU.add,
                                   _mg(t, mTB[oc].wr()))
                            sigB[sb_].read(t)
                            mTB[oc].wrote(t)
                        if oc % 2 == 1:
                            nm = "g%d" % (oc // 2)
                            ring_rel(held[nm], used[nm])
                        if oc % 4 == 3:
                            nm = "wp%d" % (oc // 4)
                            ring_rel(held[nm], used[nm])

                scr_phase()
                sp = scr_prev[0]
                if ctx.get("xo_reg") is not None:
                    xr = ctx["xo_reg"]
                    spC = ctx["sp_C"]
                    dep_gv = [_mg(spC, xr[i // 2][i % 2]) for i in range(NTM)]
                    dep_junk = _mg(spC, ctx["fg_rd"])
                    dep_saT = _mg(spC, *[xr[i][h] for i in range(3, NTM) for h in range(2)])
                    dep_R = spC
                else:
                    dep_gv = [sp] * NTM
                    dep_junk = dep_saT = dep_R = sp
                gv = v3(scr[:, 0:3072], 512)
                vnr = v3(scrR[:, 0:3072], 512)
                saT = v3(scr[:, 3072:3072 + 4 * TM], TM)
                gu = sig[:, 0, :]
                sz = sig[:, 1, :]
                s0, _ = ring_next()
                s1, _ = ring_next()
                su = {}
                sttok = {}
                junk_tok = {}
                for i0 in range(sk, NT, 2):
                    ni = min(2, NT - i0)
                    p, pd = ps_alloc()
                    tk = None
                    for ii in range(ni):
                        i = i0 + ii
                        for h, s in enumerate((s0, s1)):
                            for k in range(8):
                                tk = mm(ps[:, 2 * p + ii, h * 256:(h + 1) * 256], hT[:, k, i * 128:(i + 1) * 128],
                                        ring[:, s, k * 256:(k + 1) * 256], k == 0, k == 7,
                                        _mg(pd, *[ctx["hT_tile"][i_] for i_ in range(i0, i0 + ni)], slot_tok[s0], slot_tok[s1]) if k == 0 else None, ms=(k == 7))
                    su = _mg(su, tk)
                    for ii in range(ni):
                        i = i0 + ii
                        t = scr_touch(act(gv[:, i, :], ps[:, 2 * p + ii, :], AF.Gelu_apprx_tanh, _mg(tk, dep_gv[i], stA_B.wr()),
                                          accum=s1t[:, i:i + 1]))
                        ps_rel(p, t)
                        def fsq(e, i=i):
                            return e.scalar_tensor_tensor(out=scr[:, 6144:6656], in0=gv[:, i, :], scalar=1.0, in1=gv[:, i, :],
                                                          op0=ALU.mult, op1=ALU.mult, accum_out=s2t[:, i:i + 1])
                        t = scr_touch(S.op("dve", fsq, _mg(t, dep_junk, junk_tok)))
                        junk_tok = t
                        sttok = _mg(sttok, t)
                hTB.read(su)
                ring_rel(s0, su)
                ring_rel(s1, su)

                guB = Buf()
                yw = yTB.wr()
                ytoks = {}
                uz = {}

                def uz_jobs(fc):
                    s, _ = ring_next()
                    p, tk = fm_job(s, 0, 8, hT, hd, coff=co)
                    t = act(bv(gu[:, 0:T]), psv(p), AF.Gelu_apprx_tanh, _mg(tk, guB.wr(), sigB[0].wr()))
                    ps_rel(p, t)
                    p2, tk2 = fm_job(s, 1024, 8, hT, hd, coff=co)
                    hTB.read(tk2)
                    ring_rel(s, tk2)
                    t2 = act(bv(sz[:, 0:T]), psv(p2), AF.Silu, _mg(tk2, guB.wr(), sigB[1].wr()))
                    ps_rel(p2, t2)
                    uz[fc] = (t, t2)

                t1 = ts(mean_t[:, sk:NT], s1t[:, sk:NT], 1.0 / 512, None, ALU.mult, None, sttok)
                t1 = tt(m2t[:, sk:NT], mean_t[:, sk:NT], mean_t[:, sk:NT], ALU.mult, t1)
                t1 = stt(var_t[:, sk:NT], s2t[:, sk:NT], 1.0 / 512, m2t[:, sk:NT], ALU.mult, ALU.subtract, t1)
                t1 = rsqrt(rl3[:, sk:NT], var_t[:, sk:NT], rl[:, sk:NT], rl2[:, sk:NT], 1.0, LN_EPS, t1)
                t1 = stt(nmr[:, sk:NT], mean_t[:, sk:NT], -1.0, rl3[:, sk:NT], ALU.mult, ALU.mult, t1)
                stA_B.wrote(t1)
                uz_jobs(0)
                vtok = {}
                for i in range(sk, NT):
                    t2 = scr_touch(ts(gv[:, i, :], gv[:, i, :], rl3[:, i:i + 1], nmr[:, i:i + 1], ALU.mult, ALU.add, t1))
                    t2 = scr_touch(tt(vnr[:, i, :], gv[:, i, :], lng, ALU.mult, _mg(t2, cld, dep_R)))
                    stA_B.read(t2)
                    vtok[i] = t2
                satok = {}
                for i in range(sk, NT):
                    p, pd = ps_alloc()
                    tk = None
                    for fc in range(4):
                        tk = mm(ps[:, 2 * p + fc // 2, (fc % 2) * 256:(fc % 2 + 1) * 256], vnr[:, i, fc * 128:(fc + 1) * 128],
                                wm[:, fc * 256:(fc + 1) * 256], True, True, _mg(pd, vtok[i], wmd), ms=True)
                    scr_touch(tk)
                    et = {}
                    for hf in range(2):
                        sl = slice(hf * 64, hf * 64 + 64)
                        src = ps[sl, 2 * p:2 * p + 2, :].rearrange("p b (g c) -> p (b g) c", g=2)[:, :, hf * 128:(hf + 1) * 128]
                        t2 = tt(saT[sl, :, (i - sk) * 128:(i - sk + 1) * 128], src, bias2[sl, :, :], ALU.add, _mg(tk, dep_saT, wmd))
                        et = _mg(et, t2)
                    scr_touch(et)
                    ps_rel(p, et)
                    satok = _mg(satok, et)
                for fc in range(4):
                    if fc > 0:
                        uz_jobs(fc)
                    t, t2 = uz[fc]
                    t3 = scr_touch(tt(gu[:, 0:T], gu[:, 0:T], saT[:, fc, 0:T], ALU.mult, _mg(t, satok)))
                    t4 = scr_touch(tt(yT[:, fc, 0:T], gu[:, 0:T], sz[:, 0:T], ALU.mult, _mg(t3, t2, yw)))
                    guB.wrote(t4)
                    sigB[0].wrote(t4)
                    sigB[1].wrote(t4)
                    ytoks = _mg(ytoks, t4)
                yTB.wrote(ytoks)
                if l == 0 and j == 0:
                    dump(1, yT[:].bitcast(F32).rearrange("p a b -> p (a b)"), 3072, ytoks, [yTB])
                outproj("a", True)
                if l == 0 and j == 0:
                    dump(2, mT[:].bitcast(F32).rearrange("p a b -> p (a b)"), 6144, _mg(*[b_.rd() for b_ in mTB]), mTB)

                scr_phase()
                sp = scr_prev[0]
                yw = yTB.wr()
                ytoks = {}
                bbuf = [Buf(), Buf()]
                for fc in range(4):
                    bs_ = fc % 2
                    base = bs_ * 3584
                    tb0 = scr[:, base:base + TX]
                    yy = scr[:, base + 768:base + 768 + TX + 2]
                    acc = scr[:, base + 1544:base + 1544 + T]
                    szb = scr[:, base + 2312:base + 2312 + T]
                    wdep = _mg(sp, bbuf[bs_].wr())
                    sA, _ = ring_next()
                    p, tk = fm_job(sA, 0, 8, hT, hd, coff=0, Tc=TX)
                    t = scr_touch(act(bv(tb0, TX), psv(p, TX), AF.Copy, _mg(tk, wdep)))
                    ps_rel(p, t)
                    p, tk = fm_job(sA, 1024, 8, hT, hd, coff=0, Tc=TX)
                    ring_rel(sA, tk)
                    tc = scr_touch(cp(yy[:, 0:2], yyc[:, fc, :], _mg(wdep, yycB.rd())))
                    t = scr_touch(tt(bv(yy[:, 2:2 + TX], TX), psv(p, TX), bv(tb0, TX), ALU.mult, _mg(tk, t, wdep)))
                    ps_rel(p, t)
                    t = _mg(t, tc)
                    tcar = cp(yyc[:, fc, :], yy[:, TX:TX + 2], _mg(t, yycB.wr()))
                    yycB.wrote(tcar, fresh=False)
                    t1 = scr_touch(ts(acc, yy[:, 2 + co:2 + co + T], cw[:, fc, 2:3], cb[:, fc:fc + 1], ALU.mult, ALU.add, _mg(t, cld, wdep)))
                    t1 = scr_touch(stt(acc, yy[:, 1 + co:1 + co + T], cw[:, fc, 1:2], acc, ALU.mult, ALU.add, t1))
                    t1 = scr_touch(stt(acc, yy[:, co:co + T], cw[:, fc, 0:1], acc, ALU.mult, ALU.add, t1))
                    sB, _ = ring_next()
                    p, tk = fm_job(sB, 0, 8, hT, hd, coff=co)
                    t1 = scr_touch(tt(bv(acc), psv(p), bv(acc), ALU.mult, _mg(tk, t1)))
                    ps_rel(p, t1)
                    p, tk = fm_job(sB, 1024, 8, hT, hd, coff=co)
                    hTB.read(tk)
                    ring_rel(sB, tk)
                    t2 = scr_touch(act(bv(szb), psv(p), AF.Silu, _mg(tk, wdep)))
                    ps_rel(p, t2)
                    t3 = scr_touch(tt(yT[:, fc, 0:T], acc, szb, ALU.mult, _mg(t1, t2, yw, tcar)))
                    bbuf[bs_].wrote(t3)
                    ytoks = _mg(ytoks, t3)
                yTB.wrote(ytoks)
                if l == 0 and j == 0:
                    dump(3, yT[:].bitcast(F32).rearrange("p a b -> p (a b)"), 3072, ytoks, [yTB])
                outproj("b", False)

                scr_phase()
                sp = scr_prev[0]
                pT = yT
                szc = [scr[:, 0:TM], scr[:, TM:2 * TM]]
                ywC = yTB.wr()
                H = 16
                xa = scr[:, 1536:1536 + H + TM]
                xb_ = scr[:, 2320:2320 + H + TM]
                xraw = scr[:, 3104:3104 + H + TM]
                tmp16 = scr[:, 3888:3904]
                xrawB = Buf()
                END = H + TX
                ptok = {}
                sx = None
                for gi in range(4):
                    w = (2, 4, 8, 16)[gi]
                    if gi % 2 == 0:
                        sx, _ = ring_next()
                    p, tk = fm_job(sx, (gi % 2) * 1024, 8, hT, hd, coff=0, Tc=TX)
                    hTB.read(tk)
                    if gi % 2 == 1:
                        ring_rel(sx, tk)
                    tcy = scr_touch(cp(xraw[:, 0:H], xcc[:, gi, :], _mg(sp, xrawB.wr(), xccB.rd())))
                    t = scr_touch(act(bv(xraw[:, H:END], TX), psv(p, TX), AF.Copy, _mg(tk, sp, xrawB.wr())))
                    ps_rel(p, t)
                    t = _mg(t, tcy)
                    tcar = cp(xcc[:, gi, :], xraw[:, TX:TX + H], _mg(t, xccB.wr()))
                    xccB.wrote(tcar, fresh=False)
                    src = xraw
                    cur = _mg(t, tcar)
                    sh = 1
                    k_ = 0
                    while sh < w:
                        dst = (xa, xb_)[k_ % 2]
                        lo = 2 * sh - 1
                        cur = scr_touch(tt(dst[:, lo:END], src[:, lo:END], src[:, lo - sh:END - sh], ALU.add, _mg(cur, sp)))
                        src = dst
                        sh *= 2
                        k_ += 1
                    c0 = H + co
                    t2 = scr_touch(stt(pT[:, gi, 0:T], src[:, c0:c0 + T], 1.0 / w, xraw[:, c0:c0 + T], ALU.mult, ALU.subtract,
                                       _mg(cur, ywC)))
                    if t0 == 0:
                        cs = H + 128
                        pc = (1 - sk) * 128
                        t3 = scr_touch(tt(tmp16, src[:, cs:cs + H], invc[:, gi, :], ALU.mult, _mg(t2, cst_g)))
                        t2 = scr_touch(tt(pT[:, gi, pc:pc + H], tmp16, xraw[:, cs:cs + H], ALU.subtract, t3))
                    xrawB.wrote(t2)
                    ptok = _mg(ptok, t2)
                yw = yTB.wr()
                ytoks = {}
                sz_s = None
                szB = [Buf(), Buf()]
                for gi in range(4):
                    if gi % 2 == 0:
                        sz_s, _ = ring_next()
                    p, tk = fm_job(sz_s, (gi % 2) * 1024, 8, hT, hd, coff=co)
                    hTB.read(tk)
                    if gi % 2 == 1:
                        ring_rel(sz_s, tk)
                    t = scr_touch(act(bv(szc[gi % 2][:, 0:T]), psv(p), AF.Silu, _mg(tk, sp, szB[gi % 2].wr())))
                    ps_rel(p, t)
                    p, pd = ps_alloc()
                    tk = None
                    for ci, (c0, c1) in enumerate([(0, cwid), (cwid, T)]):
                        tk = mm(ps[:, 2 * p + ci, 0:cwid], wpool[:, gi * 128:(gi + 1) * 128], pT[:, gi, c0:c1], True, True,
                                _mg(pd, ptok, cst_g), ms=True)
                    scr_touch(tk)
                    t2 = scr_touch(stt(bv(yT[:, gi, 0:T]), psv(p), psc[:, gi:gi + 1], bv(szc[gi % 2][:, 0:T]), ALU.mult, ALU.mult,
                                       _mg(tk, t, yw, cld)))
                    ps_rel(p, t2)
                    szB[gi % 2].wrote(t2)
                    ytoks = _mg(ytoks, t2)
                yTB.wrote(ytoks)
                if l == 0 and j == 0:
                    dump(4, yT[:].bitcast(F32).rearrange("p a b -> p (a b)"), 3072, ytoks, [yTB])
                nxt = passes[n + 1] if n + 1 < len(passes) else None
                need_setup = nxt is not None and nxt[0] != l
                if need_setup:
                    setup_A(nxt[0], all_now())
                outproj("c", False)
                if need_setup:
                    setup_B(nxt[0])
                if l == 0 and j == 0:
                    dump(5, mT[:].bitcast(F32).rearrange("p a b -> p (a b)"), 6144, _mg(*[b_.rd() for b_ in mTB]), mTB)

                scr_phase()
                sp = scr_prev[0]
                xo = v3(scr[:, 0:NTM * 1024], 1024)
                fgb = scr[:, 6144:7168]
                md = _mg(*[b_.rd() for b_ in mTB])
                p0_list = []
                if nxt is not None:
                    nNT = SUPER[nxt[1]]
                    nt0 = sum(SUPER[:nxt[1]])
                    p0_list = [(nxt[0], nt0 + i, i) for i in range(nNT)]
                hdeps_n = hTB.wr()
                hw = {}
                stA = phase0_A(*p0_list.pop(0)) if p0_list else None
                if l == DEPTH - 1:
                    tfg = scr_touch(dma("pool", fgb, fg_d, "fgl", sp))
                xo_tok = {}
                x1_tok = {}
                xo_reg = [[{}, {}] for _ in range(NTM)]
                ctx["sp_C"] = sp
                fg_rd = dict(tfg) if l == DEPTH - 1 else {}
                add_tok = {}
                ctx["fg_rd"] = fg_rd
                pend = []

                def fin_a(i, dep):
                    junk = sig[:].rearrange("p a b -> p (a b)")[:, 0:D]
                    t = scr_touch(act(junk, xo[:, i, :], AF.Square, _mg(dep, sigB[0].wr(), sigB[1].wr(), stat2B[i].wr()),
                                      accum=ssq2[:, i:i + 1]))
                    sigB[0].w = _mg(sigB[0].w, t)
                    sigB[1].w = _mg(sigB[1].w, t)
                    t = rsqrt(rst2[:, i:i + 1], ssq2[:, i:i + 1], tmpa2[:, i:i + 1], tmpb2[:, i:i + 1], 1.0 / D, RMS_EPS, t)
                    pend.append((i, t))

                def fin_b(i, t):
                    t = scr_touch(stt(xo[:, i, :], xo[:, i, :], rst2[:, i:i + 1], fgb, ALU.mult, ALU.mult, _mg(t, tfg)))
                    stat2B[i].wrote(t)
                    ctx["fg_rd"] = _mg(ctx["fg_rd"], t)
                    if t0 + i >= 1:
                        r0 = (t0 + i - 1) * 128
                        t = scr_touch(dma("sp", out_d[r0:r0 + 128, :], xo[:, i, :], "so%d" % i, t))
                        ctx["out_toks"] = _mg(ctx.get("out_toks"), t)
                    xo_reg[i][0] = xo_reg[i][1] = t

                for half in range(2):
                    sa, _ = ring_next()
                    sb2, _ = ring_next()
                    src = xsrc[(t0 + sk) * 128:(t0 + NT) * 128, half * 512:(half + 1) * 512].rearrange("(i p) d -> p i d", p=128)
                    tx = scr_touch(dma("pool", xo[:, sk:NT, half * 512:(half + 1) * 512], src, "xin%d" % half,
                                       _mg(sp, x1_st.get((l - 1, j)) if l > 0 else None)))
                    su = {}
                    for i0 in range(sk, NT, 2):
                        ni = min(2, NT - i0)
                        p, pd = ps_alloc()
                        tk = None
                        for ii in range(ni):
                            i = i0 + ii
                            for h, s in enumerate((sa, sb2)):
                                for k in range(8):
                                    tk = mm(ps[:, 2 * p + ii, h * 256:(h + 1) * 256], mT[:, k, (i - sk) * 128:(i - sk + 1) * 128],
                                            ring[:, s, k * 256:(k + 1) * 256], k == 0, k == 7,
                                            _mg(pd, md, slot_tok[sa], slot_tok[sb2]) if k == 0 else None, ms=(k == 7))
                        su = _mg(su, tk)
                        do_b = stA is not None
                        if do_b:
                            nstA = phase0_A(*p0_list.pop(0)) if p0_list else None
                        t = scr_touch(tt(xo[:, i0:i0 + ni, half * 512:(half + 1) * 512], ps[:, 2 * p:2 * p + ni, :],
                                         xo[:, i0:i0 + ni, half * 512:(half + 1) * 512], ALU.add, _mg(tk, tx, sp)))
                        ps_rel(p, t)
                        xo_tok = _mg(xo_tok, t)
                        add_tok[(half, i0)] = t
                        if do_b:
                            hw = _mg(hw, phase0_B(stA, hdeps_n))
                            stA = nstA
                        if l == DEPTH - 1 and half == 1:
                            while len(pend) > 0:
                                fin_b(*pend.pop(0))
                            for i_ in range(i0, i0 + ni):
                                fin_a(i_, _mg(t, add_tok[(0, i0)]))
                        if l < DEPTH - 1:
                            dst = x1_d[(t0 + i0) * 128:(t0 + i0 + ni) * 128, half * 512:(half + 1) * 512].rearrange(
                                "(i p) d -> p i d", p=128)
                            t_ = scr_touch(dma("sp", dst, xo[:, i0:i0 + ni, half * 512:(half + 1) * 512], "sx%d" % (half * 3 + i0 // 2), t))
                            x1_tok = _mg(x1_tok, t_)
                            for i_ in range(i0, i0 + ni):
                                xo_reg[i_][half] = t_
                    for b_ in mTB:
                        b_.read(su)
                    ring_rel(sa, su)
                    ring_rel(sb2, su)
                while stA is not None:
                    nstA = phase0_A(*p0_list.pop(0)) if p0_list else None
                    hw = _mg(hw, phase0_B(stA, hdeps_n))
                    stA = nstA
                if nxt is not None:
                    hTB.wrote(hw)
                if l == 0 and j == 0:
                    dump(6, scr[:, 0:6144], 6144, xo_tok)
                ctx["xo_reg"] = xo_reg
                if l < DEPTH - 1:
                    x1_st[(l, j)] = x1_tok
                else:
                    while pend:
                        fin_b(*pend.pop(0))


        S.op("pool", lambda e: e.memset(tmp8[:, 0:1], 0.0), _mg(ctx.get("out_toks"), dbg_toks), ms=False)

        with nc.Block() as block:
            @block.sync
            def _(e):
                S.run("sp", e, sems)

            @block.tensor
            def _(e):
                S.run("pe", e, sems)

            @block.scalar
            def _(e):
                S.run("act", e, sems)

            @block.vector
            def _(e):
                S.run("dve", e, sems)

            @block.gpsimd
            def _(e):
                S.run("pool", e, sems)
    return nc


def _fm_unit(W, c0):
    K = W.shape[0]
    return W[:, c0:c0 + 128].reshape(K // 128, 128, 128).transpose(1, 0, 2).reshape(128, -1)


def _tm_unit(W, c0):
    K = W.shape[0]
    return W[:, c0:c0 + 256].reshape(K // 128, 128, 256).transpose(1, 0, 2).reshape(128, -1)


def _weight_stream(w_in, w_pa, w_pb, w_pc, w_o):
    slots = []

    def outproj(Wp, cg0):
        def g(i):
            return np.concatenate([_fm_unit(w_in, cg0 + (2 * i) * 128), _fm_unit(w_in, cg0 + (2 * i + 1) * 128)], axis=1)

        def wp(i):
            return np.concatenate([_fm_unit(Wp, (4 * i + q) * 128) for q in range(4)], axis=1)
        return [g(0), wp(0), g(1), g(2), wp(1), g(3)]

    slots += [_tm_unit(w_in, C_V), _tm_unit(w_in, C_V + 256)]
    for fc in range(4):
        slots.append(np.concatenate([_fm_unit(w_in, C_U + fc * 128), _fm_unit(w_in, C_ZA + fc * 128)], axis=1))
    slots += outproj(w_pa, C_GA)
    for fc in range(4):
        slots.append(np.concatenate([_fm_unit(w_in, C_XB + fc * 128), _fm_unit(w_in, C_CG + fc * 128)], axis=1))
        slots.append(np.concatenate([_fm_unit(w_in, C_BG + fc * 128), _fm_unit(w_in, C_ZB + fc * 128)], axis=1))
    slots += outproj(w_pb, C_GB)
    slots.append(np.concatenate([_fm_unit(w_in, C_XC), _fm_unit(w_in, C_XC + 128)], axis=1))
    slots.append(np.concatenate([_fm_unit(w_in, C_XC + 256), _fm_unit(w_in, C_XC + 384)], axis=1))
    slots.append(np.concatenate([_fm_unit(w_in, C_ZC), _fm_unit(w_in, C_ZC + 128)], axis=1))
    slots.append(np.concatenate([_fm_unit(w_in, C_ZC + 256), _fm_unit(w_in, C_ZC + 384)], axis=1))
    slots += outproj(w_pc, C_GC)
    for cb in range(4):
        slots.append(_tm_unit(w_o, cb * 256))
    assert len(slots) == NSLOT_L
    return np.stack(slots, axis=0)


def _band_consts():
    s = np.arange(128)[:, None]
    t = np.arange(128)[None, :]
    band = np.zeros((128, 4, 128), np.float32)
    bandp = np.zeros((128, 4, 128), np.float32)
    bandf = np.zeros((128, 4, 128), np.float32)
    for gi, w in enumerate((2, 4, 8, 16)):
        inw = (s <= t) & (s > t - w)
        band[:, gi, :] = np.where(inw, np.float32(1.0 / w), 0.0) - np.where(s == t, 1.0, 0.0)
        inwp = ((s - 128) > t - w)
        bandp[:, gi, :] = np.where(inwp, np.float32(1.0 / w), 0.0)
        cnt = np.minimum(t + 1, w).astype(np.float32)
        bandf[:, gi, :] = np.where(inw, np.float32(1.0) / cnt, 0.0) - np.where(s == t, 1.0, 0.0)
    return band.astype(np.float32), bandp.astype(np.float32), bandf.astype(np.float32)


_NC_CACHE = {}


def kernel(x, norm_g, w_in, ln_g, ln_b, w_s, b_s, conv_w, conv_b, w_pool, pool_scale,
           w_pa, w_pb, w_pc, w_o, final_g):
    f = lambda a: np.ascontiguousarray(np.asarray(a, dtype=np.float32))
    x, norm_g, w_in, ln_g, ln_b, w_s, b_s = map(f, (x, norm_g, w_in, ln_g, ln_b, w_s, b_s))
    conv_w, conv_b, w_pool, pool_scale = map(f, (conv_w, conv_b, w_pool, pool_scale))
    w_pa, w_pb, w_pc, w_o, final_g = map(f, (w_pa, w_pb, w_pc, w_o, final_g))
    B, SEQ, _ = x.shape
    cores_per_b = NCORES // B

    ws = np.concatenate([_weight_stream(w_in[l], w_pa[l], w_pb[l], w_pc[l], w_o[l]) for l in range(DEPTH)], axis=0)
    ws = np.ascontiguousarray(ws, dtype=np.float32)

    cl = np.zeros((DEPTH, 128, CL), np.float32)
    for l in range(DEPTH):
        cl[l, :, CL_G:CL_G + 8] = norm_g[l].reshape(8, 128).T
        cl[l, :, CL_LNB:CL_LNB + 4] = ln_b[l].reshape(4, 128).T
        cl[l, :, CL_CW:CL_CW + 12] = conv_w[l].reshape(3, 4, 128).transpose(2, 1, 0).reshape(128, 12)
        cl[l, :, CL_CB:CL_CB + 4] = conv_b[l].reshape(4, 128).T
        cl[l, :, CL_PS:CL_PS + 4] = pool_scale[l].reshape(4, 128).T
        cl[l, :, CL_LNG:CL_LNG + 4] = ln_g[l].reshape(4, 128).T
        cl[l, :, CL_LNGB:CL_LNGB + 512] = np.broadcast_to(ln_g[l][None, :], (128, 512))
        bs = b_s[l].reshape(4, 2, 1, 128)
        cl[l, :, CL_BS:CL_BS + 512] = np.broadcast_to(bs, (4, 2, 64, 128)).transpose(1, 2, 0, 3).reshape(128, 512)
        cl[l, :, CL_WS:CL_WS + 1024] = w_s[l].transpose(2, 0, 1).reshape(128, 1024)
    band, bandp, bandf = _band_consts()
    ones = np.ones((128, 128), np.float32)
    def cg_for(first):
        inv = np.zeros((4, 16), np.float32)
        for gi_, w_ in enumerate((2, 4, 8, 16)):
            cnt_ = np.minimum(np.arange(1, 17), w_) if first else np.full(16, w_)
            inv[gi_] = (np.float32(1.0) / cnt_.astype(np.float32)).astype(np.float32)
        return np.ascontiguousarray(np.concatenate(
            [np.eye(128, dtype=np.float32), np.triu(np.ones((128, 128), np.float32)),
             np.broadcast_to(inv.reshape(1, 64), (128, 64))], axis=1), dtype=np.float32)

    cg_first, cg_other = cg_for(True), cg_for(False)
    fg = np.ascontiguousarray(np.broadcast_to(final_g[None, :], (128, D)), dtype=np.float32)

    def cr_for(first):
        parts = [w_pool[l].transpose(1, 0, 2).reshape(128, 512) for l in range(DEPTH)]
        parts += [band.reshape(128, 512), bandp.reshape(128, 512),
                  (bandf if first else band).reshape(128, 512), ones]
        return np.ascontiguousarray(np.concatenate(parts, axis=1), dtype=np.float32)

    cr_first, cr_other = cr_for(True), cr_for(False)

    in_maps = []
    for c in range(NCORES):
        b = c // cores_per_b
        s0 = (c % cores_per_b) * TOK_CORE
        xc = np.zeros((NTILES * 128, D), np.float32)
        if s0 == 0:
            xc[128:] = x[b, 0:TOK_CORE]
        else:
            xc[:] = x[b, s0 - 128:s0 + TOK_CORE]
        in_maps.append({"x": xc, "ws": ws, "cl": cl, "cr": cr_first if s0 == 0 else cr_other, "cg": cg_first if s0 == 0 else cg_other, "fg": fg})

    if "nc" not in _NC_CACHE:
        _NC_CACHE["nc"] = build_program()
    nc = _NC_CACHE["nc"]
    res = run_bass_kernel_spmd(nc, in_maps, core_ids=list(range(NCORES)))
    if _DBG:
        _NC_CACHE["dbg"] = [res.results[c]["dbg"] for c in range(NCORES)]
    out = np.empty((B, SEQ, D), np.float32)
    for c in range(NCORES):
        b = c // cores_per_b
        s0 = (c % cores_per_b) * TOK_CORE
        out[b, s0:s0 + TOK_CORE] = res.results[c]["out"]
    return out
```

```python
import numpy as np
from contextlib import ExitStack
import concourse.bass as bass
import concourse.mybir as mybir
from concourse.bass_utils import run_bass_kernel_spmd

F32 = mybir.dt.float32
F32R = mybir.dt.float32r
AF = mybir.ActivationFunctionType
ALU = mybir.AluOpType

D = 1024
NCORES = 8
TOK_CORE = 2048
NTILES = 17
SUPER = [6, 6, 5]
NTM = 6
TM = NTM * 128
DEPTH = 2
HALO_SKIP = True
_DBG = False
SLOT = 2048
NSLOT_L = 40
RING = 6
RMS_EPS = 1e-6
LN_EPS = 1e-5

C_U, C_V, C_ZA, C_XB, C_BG, C_CG, C_ZB, C_XC, C_ZC, C_GA, C_GB, C_GC = (
    0, 512, 1024, 1536, 2048, 2560, 3072, 3584, 4096, 4608, 5632, 6656)

CL_G, CL_LNB, CL_CW, CL_CB, CL_PS = 0, 8, 12, 24, 28
CL_LNG, CL_BS, CL_WS = 32, 36, 548
CL_LNGB = 1572
CL = 2084
CR_L = 512
CGW = 320


def _mg(*ds):
    r = {}
    for d in ds:
        if not d:
            continue
        for k, v in d.items():
            if r.get(k, 0) < v:
                r[k] = v
    return r


class Sched:
    def __init__(self):
        self.ops = {}
        self.cnt = {}

    def op(self, eng, fn, deps=None, sem=None, amt=1, ms=True):
        tok = None
        semname = None
        if ms:
            semname = sem or eng
            self.cnt[semname] = self.cnt.get(semname, 0) + amt
            tok = {semname: self.cnt[semname]}
        self.ops.setdefault(eng, []).append((deps or {}, fn, semname, amt))
        return tok

    def run(self, eng_name, eng, sems):
        known = {}
        for deps, fn, semname, amt in self.ops.get(eng_name, []):
            for s, v in deps.items():
                if known.get(s, 0) >= v:
                    continue
                eng.wait_ge(sems[s], v)
                known[s] = v
            ins = fn(eng)
            if semname is not None:
                ins.then_inc(sems[semname], amt)


class Buf:
    def __init__(self):
        self.w = {}
        self.r = {}

    def rd(self):
        return self.w

    def wr(self):
        return _mg(self.w, self.r)

    def wrote(self, tok, fresh=True):
        if fresh:
            self.w = dict(tok)
            self.r = {}
        else:
            self.w = _mg(self.w, tok)

    def read(self, tok):
        self.r = _mg(self.r, tok)


def build_program():
    nc = bass.Bass("TRN2", target_bir_lowering=False)
    nc.dge_precook = False
    x_d = nc.dram_tensor("x", [NTILES * 128, D], F32, kind="ExternalInput").ap()
    ws_d = nc.dram_tensor("ws", [DEPTH * NSLOT_L, 128, SLOT], F32, kind="ExternalInput").ap()
    cl_d = nc.dram_tensor("cl", [DEPTH, 128, CL], F32, kind="ExternalInput").ap()
    cr_d = nc.dram_tensor("cr", [128, DEPTH * CR_L + 1664], F32, kind="ExternalInput").ap()
    cg_d = nc.dram_tensor("cg", [128, CGW], F32, kind="ExternalInput").ap()
    fg_d = nc.dram_tensor("fg", [128, D], F32, kind="ExternalInput").ap()
    out_d = nc.dram_tensor("out", [TOK_CORE, D], F32, kind="ExternalOutput").ap()
    x1_d = nc.dram_tensor("x1", [NTILES * 128, D], F32).ap()
    dbg_d = nc.dram_tensor("dbg", [8, 128, 6144], F32, kind="ExternalOutput").ap() if _DBG else None

    S = Sched()
    es = ExitStack()

    def sb(name, shape, dt):
        return es.enter_context(nc.sbuf_tensor(name, shape, dt))

    with es:
        hT = sb("hT", [128, 8, TM], F32R)
        mT = sb("mT", [128, 8, TM], F32R)
        yT = sb("yT", [128, 4, TM], F32R)
        ring = sb("ring", [128, RING, SLOT], F32R)
        scr = sb("scr", [128, 7168], F32)
        scrR = sb("scrR", [128, 3584], F32R)
        xt = sb("xt", [128, 2, D], F32)
        xs = sb("xs", [128, 2, D], F32)
        sig = sb("sig", [128, 2, TM], F32)
        clt = sb("clt", [128, CL], F32)
        crt = sb("crt", [128, DEPTH * CR_L + 1664], F32R)
        cgt = sb("cgt", [128, CGW], F32)
        wm = sb("wm", [128, 1024], F32R)
        bias2 = sb("bias2", [128, 4, 128], F32)
        ssq = sb("ssq", [128, 8], F32)
        rst = sb("rst", [128, 8], F32)
        tmp8 = sb("tmp8", [128, 8], F32)
        bnst = sb("bnst", [128, NTM, 6], F32)
        mv = sb("mv", [128, NTM, 2], F32)
        rl = sb("rl", [128, NTM], F32)
        rl2 = sb("rl2", [128, NTM], F32)
        rl3 = sb("rl3", [128, NTM], F32)
        nmr = sb("nmr", [128, NTM], F32)
        s1t = sb("s1t", [128, NTM], F32)
        s2t = sb("s2t", [128, NTM], F32)
        mean_t = sb("mean_t", [128, NTM], F32)
        m2t = sb("m2t", [128, NTM], F32)
        var_t = sb("var_t", [128, NTM], F32)
        tmp9 = sb("tmp9", [128, 8], F32)
        yyc = sb("yyc", [128, 4, 2], F32)
        xcc = sb("xcc", [128, 4, 16], F32)
        ps = es.enter_context(nc.psum_tensor("ps", [128, 8, 512], F32))

        dbg_toks = {}
        sem_names = ["dbgs", "pe", "act", "dve", "pool", "cg", "cr", "cl", "xl0", "xl1", "xin0", "xin1", "sx0", "sx1", "sx2", "sx3", "sx4", "sx5", "so0", "so1", "so2", "so3", "so4", "so5", "sto", "fgl"] + \
            ["w%d" % s for s in range(RING)]
        sems = {n: es.enter_context(nc.semaphore(n)) for n in sem_names}

        def mm(out, lhsT, rhs, start, stop, deps=None, ms=False):
            return S.op("pe", lambda e: e.matmul(out, lhsT, rhs, start=start, stop=stop), deps, ms=ms)

        def tr(out, in_, ident, deps=None, ms=False):
            return S.op("pe", lambda e: e.transpose(out, in_, ident), deps, ms=ms)

        def act(out, in_, func, deps=None, scale=None, accum=None):
            def f(e):
                kw = {}
                if scale is not None:
                    kw["scale"] = scale
                if accum is not None:
                    kw["accum_out"] = accum
                return e.activation(out=out, in_=in_, func=func, **kw)
            return S.op("act", f, deps)

        def ts(out, in0, s1, s2, op0, op1=None, deps=None):
            def f(e):
                if op1 is None:
                    return e.tensor_scalar(out=out, in0=in0, scalar1=s1, scalar2=None, op0=op0)
                return e.tensor_scalar(out=out, in0=in0, scalar1=s1, scalar2=s2, op0=op0, op1=op1)
            return S.op("dve", f, deps)

        def tt(out, in0, in1, op, deps=None):
            return S.op("dve", lambda e: e.tensor_tensor(out=out, in0=in0, in1=in1, op=op), deps)

        def stt(out, in0, sc, in1, op0, op1, deps=None):
            return S.op("dve", lambda e: e.scalar_tensor_tensor(out=out, in0=in0, scalar=sc, in1=in1, op0=op0, op1=op1), deps)

        def cp(out, in_, deps=None):
            return S.op("dve", lambda e: e.tensor_copy(out=out, in_=in_), deps)

        def rsqrt(dst, src, ta, tb, scale, eps, deps):
            t = ts(ta, src, scale, eps, ALU.mult, ALU.add, deps)
            t = act(tb, ta, AF.Sqrt, t)
            return S.op("dve", lambda e: e.reciprocal(out=dst, in_=tb), t)

        def dma(q, out, in_, sem, deps=None):
            return S.op(q, lambda e: e.dma_start(out=out, in_=in_), deps, sem=sem, amt=16)

        def v3(ap, b):
            return ap.rearrange("p (a b) -> p a b", b=b)

        def dump(idx, ap, width, deps, bufs=()):
            if not _DBG:
                return
            t = dma("pool", dbg_d[idx, :, 0:width], ap, "dbgs", deps)
            for b_ in bufs:
                b_.read(t)
            scr_touch(t)
            dbg_toks.update(_mg(dbg_toks, t))

        ps_rd = [dict() for _ in range(4)]
        ps_nxt = [0]

        def ps_alloc():
            p = ps_nxt[0]
            ps_nxt[0] = (p + 1) % 4
            d = ps_rd[p]
            ps_rd[p] = {}
            return p, d

        def ps_rel(p, tok):
            ps_rd[p] = _mg(ps_rd[p], tok)

        total_items = DEPTH * len(SUPER) * NSLOT_L
        rg = {"q": 0, "c": 0}
        slot_rd = [dict() for _ in range(RING)]
        slot_tok = [None] * RING

        slot_free = [True] * RING

        def ring_issue():
            while rg["q"] < total_items and slot_free[rg["q"] % RING]:
                q = rg["q"]
                s = q % RING
                layer = q // (len(SUPER) * NSLOT_L)
                item = q % NSLOT_L
                src = ws_d[layer * NSLOT_L + item].bitcast(F32R)
                slot_tok[s] = dma("sp", ring[:, s, :], src, "w%d" % s, slot_rd[s])
                slot_rd[s] = {}
                slot_free[s] = False
                rg["q"] = q + 1

        def ring_next():
            c = rg["c"]
            rg["c"] = c + 1
            s = c % RING
            return s, slot_tok[s]

        def ring_rel(s, tok):
            slot_rd[s] = _mg(slot_rd[s], tok)
            slot_free[s] = True
            ring_issue()

        cst_g = _mg(dma("pool", cgt[:], cg_d, "cg"), dma("pool", crt[:], cr_d.bitcast(F32R), "cr"))
        ident = cgt[:, 0:128]
        maskT = cgt[:, 128:256]
        invc = v3(cgt[:, 256:320], 16)
        GB = DEPTH * CR_L
        band = crt[:, GB:GB + 512]
        bandp = crt[:, GB + 512:GB + 1024]
        bandf = crt[:, GB + 1024:GB + 1536]
        ones = crt[:, GB + 1536:GB + 1664]

        ring_issue()

        clB = Buf()
        wmB = Buf()
        hTB = Buf()
        mTB = [Buf() for _ in range(8)]
        yTB = Buf()
        xtB = [Buf(), Buf()]
        xsB = [Buf(), Buf()]
        sigB = [Buf(), Buf()]
        statB = [Buf() for _ in range(8)]
        yycB = Buf()
        xccB = Buf()
        scr_prev = [dict()]
        scr_cur = [dict()]
        x1_st = {}
        out_toks = {}

        def scr_touch(tok):
            scr_cur[0] = _mg(scr_cur[0], tok)
            return tok

        def scr_phase():
            scr_prev[0] = _mg(scr_prev[0], scr_cur[0])
            scr_cur[0] = {}

        xl_cnt = [0]


        ssq2 = sb("ssq2", [128, 8], F32)
        rst2 = sb("rst2", [128, 8], F32)
        tmpa2 = sb("tmpa2", [128, 8], F32)
        tmpb2 = sb("tmpb2", [128, 8], F32)
        stat2B = [Buf() for _ in range(8)]
        junkB = Buf()
        stA_B = Buf()
        g_fm = clt[:, CL_G:CL_G + 8]
        lnb = clt[:, CL_LNB:CL_LNB + 4]
        cw = v3(clt[:, CL_CW:CL_CW + 12], 3)
        cb = clt[:, CL_CB:CL_CB + 4]
        psc = clt[:, CL_PS:CL_PS + 4]
        lngf = clt[:, CL_LNG:CL_LNG + 4]
        lng = clt[:, CL_LNGB:CL_LNGB + 512]
        bsb = v3(clt[:, CL_BS:CL_BS + 512], 128)
        wsT = clt[:, CL_WS:CL_WS + 1024]
        ctx = {}

        def all_now():
            return {e_: S.cnt[e_] for e_ in ("pe", "act", "dve") if S.cnt.get(e_)}

        def setup_A(l, bar=None):
            bar = bar or {}
            t = dma("pool", clt[:], cl_d[l], "cl", bar)
            cld = _mg(t, cst_g)
            ctx["cld"] = cld
            ctx["bar"] = bar
            wtoks = {}
            for g in range(8):
                tk = tt(wm[:, g * 128:(g + 1) * 128], wsT[:, g * 128:(g + 1) * 128], maskT, ALU.mult, _mg(cld, bar))
                wtoks = _mg(wtoks, tk)
            ctx["wtoks"] = wtoks
            t = S.op("dve", lambda e: e.memset(yyc[:], 0.0), bar)
            yycB.wrote(t)
            t = S.op("dve", lambda e: e.memset(xcc[:], 0.0), bar)
            xccB.wrote(t)

        def setup_B(l):
            cld = ctx["cld"]
            bar = ctx["bar"]
            wtoks = ctx["wtoks"]
            p, pd = ps_alloc()
            tk = None
            for g in range(8):
                tk = mm(ps[:, 2 * p + g // 4, (g % 4) * 128:(g % 4 + 1) * 128], ones, wm[:, g * 128:(g + 1) * 128],
                        True, True, _mg(pd, wtoks), ms=True)
            btok = {}
            for fc in range(4):
                for hf in range(2):
                    g = 2 * fc + hf
                    sl = slice(hf * 64, hf * 64 + 64)
                    t2 = stt(bias2[sl, fc, :], ps[sl, 2 * p + g // 4, (g % 4) * 128:(g % 4 + 1) * 128],
                             lnb[sl, fc:fc + 1], bsb[sl, fc, :], ALU.mult, ALU.add, _mg(tk, cld, bar))
                    btok = _mg(btok, t2)
            ps_rel(p, btok)
            ctx["wmd"] = _mg(wtoks, btok)

        def st_of(gt):
            return [k for k in range(len(SUPER)) if sum(SUPER[:k]) <= gt < sum(SUPER[:k + 1])][0]

        def phase0_A(l, gt, i):
            xsrc = x_d if l == 0 else x1_d
            b = xl_cnt[0] % 2
            xl_cnt[0] += 1
            ld_dep = xtB[b].wr()
            if l > 0:
                ld_dep = _mg(ld_dep, x1_st[(l - 1, st_of(gt))])
            t = dma("pool", xt[:, b, :], xsrc[gt * 128:(gt + 1) * 128, :], "xl%d" % b, ld_dep)
            xtB[b].wrote(t)
            t = act(xs[:, b, :], xt[:, b, :], AF.Square, _mg(xtB[b].rd(), xsB[b].wr(), statB[i].wr()), accum=ssq[:, i:i + 1])
            xtB[b].read(t)
            t = rsqrt(rst[:, i:i + 1], ssq[:, i:i + 1], tmp8[:, i:i + 1], tmp9[:, i:i + 1], 1.0 / D, RMS_EPS, t)
            t = act(xs[:, b, :], xt[:, b, :], AF.Copy, t, scale=rst[:, i:i + 1])
            statB[i].wrote(t)
            xtB[b].read(t)
            xsB[b].wrote(t)
            return (b, i)

        def phase0_B(st, hdeps):
            b, i = st
            cld = ctx["cld"]
            p, pd = ps_alloc()
            tk = None
            for k in range(8):
                tk = tr(ps[:, 2 * p + k // 4, (k % 4) * 128:(k % 4 + 1) * 128], xs[:, b, k * 128:(k + 1) * 128], ident,
                        _mg(pd, xsB[b].rd(), cst_g) if k == 0 else None, ms=(k == 7))
            xsB[b].read(tk)
            et = {}
            for k0 in (0, 4):
                t2 = tt(hT[:, k0:k0 + 4, i * 128:(i + 1) * 128], v3(ps[:, 2 * p + k0 // 4, :], 128),
                        g_fm[:, k0:k0 + 4].rearrange("p (k o) -> p k o", o=1).to_broadcast([128, 4, 128]),
                        ALU.mult, _mg(tk, hdeps, cld))
                et = _mg(et, t2)
            ps_rel(p, et)
            ctx.setdefault("hT_tile", {})[i] = _mg(et, hdeps)
            return et

        passes = [(l, j) for l in range(DEPTH) for j in range(len(SUPER))]
        setup_A(0)
        setup_B(0)
        hw = {}
        hdeps0 = hTB.wr()
        st_ = phase0_A(0, 0, 0)
        for i in range(SUPER[0]):
            nst_ = phase0_A(0, i + 1, i + 1) if i + 1 < SUPER[0] else None
            hw = _mg(hw, phase0_B(st_, hdeps0))
            st_ = nst_
        hTB.wrote(hw)

        for n, (l, j) in enumerate(passes):
            if True:
                NT = SUPER[j]
                t0 = sum(SUPER[:j])
                xsrc = x_d if l == 0 else x1_d
                cld = ctx["cld"]
                wmd = ctx["wmd"]
                wpool = crt[:, l * CR_L:(l + 1) * CR_L]
                sk = 1 if (l > 0 and l == DEPTH - 1 and j == 0 and HALO_SKIP) else 0
                co = sk * 128
                TX = NT * 128
                T = (NT - sk) * 128
                cwid = T // 2
                hd = hTB.rd()
                if l == 0 and j == 0:
                    dump(0, hT[:].bitcast(F32).rearrange("p a b -> p (a b)"), 6144, hd, [hTB])

                def fm_job(s, off, kch, rhs3, rdeps, coff=0, Tc=None):
                    Tc = Tc or T
                    cw_ = Tc // 2
                    p, pd = ps_alloc()
                    tk = None
                    for ci, (c0, c1) in enumerate([(0, cw_), (cw_, Tc)]):
                        for k in range(kch):
                            tk = mm(ps[:, 2 * p + ci, 0:cw_], ring[:, s, off + k * 128:off + (k + 1) * 128],
                                    rhs3[:, k, coff + c0:coff + c1], k == 0, k == kch - 1,
                                    _mg(pd, rdeps, slot_tok[s]) if (k == 0) else None, ms=(k == kch - 1))
                    return p, tk

                def psv(p, Tc=None):
                    return ps[:, 2 * p:2 * p + 2, 0:(Tc or T) // 2]

                def bv(ap2, Tc=None):
                    return v3(ap2, (Tc or T) // 2)

                def outproj(br, first):
                    yd = yTB.rd()
                    order = ["g0", "wp0", "g1", "g2", "wp1", "g3"]
                    it = iter(order)
                    held = {}
                    used = {}

                    def get(name):
                        while name not in held:
                            nm = next(it)
                            s, _ = ring_next()
                            held[nm] = s
                            used[nm] = {}
                        return held[name]

                    for oc in range(8):
                        gs = get("g%d" % (oc // 2))
                        ws_ = get("wp%d" % (oc // 4))
                        p, tk = fm_job(gs, (oc % 2) * 1024, 8, hT, hd, coff=co)
                        used["g%d" % (oc // 2)] = _mg(used["g%d" % (oc // 2)], tk)
                        hTB.read(tk)
                        sb_ = oc % 2
                        t = act(bv(sig[:, sb_, 0:T]), psv(p), AF.Sigmoid, _mg(tk, sigB[sb_].wr()))
                        ps_rel(p, t)
                        sigB[sb_].wrote(t)
                        p, tk = fm_job(ws_, (oc % 4) * 512, 4, yT, yd)
                        used["wp%d" % (oc // 4)] = _mg(used["wp%d" % (oc // 4)], tk)
                        yTB.read(tk)
                        if first:
                            t = tt(bv(mT[:, oc, 0:T]), psv(p), bv(sig[:, sb_, 0:T]), ALU.mult,
                                   _mg(tk, sigB[sb_].rd(), mTB[oc].wr()))
                            ps_rel(p, t)
                            sigB[sb_].read(t)
                            mTB[oc].wrote(t)
                        else:
                            t = tt(bv(sig[:, sb_, 0:T]), psv(p), bv(sig[:, sb_, 0:T]), ALU.mult,
                                   _mg(tk, sigB[sb_].rd()))
                            ps_rel(p, t)
                            t = tt(mT[:, oc, 0:T], mT[:, oc, 0:T].bitcast(F32), sig[:, sb_, 0:T], ALU.add,
                                   _mg(t, mTB[oc].wr()))
                            sigB[sb_].read(t)
                            mTB[oc].wrote(t)
                        if oc % 2 == 1:
                            nm = "g%d" % (oc // 2)
                            ring_rel(held[nm], used[nm])
                        if oc % 4 == 3:
                            nm = "wp%d" % (oc // 4)
                            ring_rel(held[nm], used[nm])

                scr_phase()
                sp = scr_prev[0]
                if ctx.get("xo_reg") is not None:
                    xr = ctx["xo_reg"]
                    spC = ctx["sp_C"]
                    dep_gv = [_mg(spC, xr[i // 2][i % 2]) for i in range(NTM)]
                    dep_junk = _mg(spC, ctx["fg_rd"])
                    dep_saT = _mg(spC, *[xr[i][h] for i in range(3, NTM) for h in range(2)])
                    dep_R = spC
                else:
                    dep_gv = [sp] * NTM
                    dep_junk = dep_saT = dep_R = sp
                gv = v3(scr[:, 0:3072], 512)
                vnr = v3(scrR[:, 0:3072], 512)
                saT = v3(scr[:, 3072:3072 + 4 * TM], TM)
                gu = sig[:, 0, :]
                sz = sig[:, 1, :]
                s0, _ = ring_next()
                s1, _ = ring_next()
                su = {}
                sttok = {}
                junk_tok = {}
                for i0 in range(sk, NT, 2):
                    ni = min(2, NT - i0)
                    p, pd = ps_alloc()
                    tk = None
                    for ii in range(ni):
                        i = i0 + ii
                        for h, s in enumerate((s0, s1)):
                            for k in range(8):
                                tk = mm(ps[:, 2 * p + ii, h * 256:(h + 1) * 256], hT[:, k, i * 128:(i + 1) * 128],
                                        ring[:, s, k * 256:(k + 1) * 256], k == 0, k == 7,
                                        _mg(pd, *[ctx["hT_tile"][i_] for i_ in range(i0, i0 + ni)], slot_tok[s0], slot_tok[s1]) if k == 0 else None, ms=(k == 7))
                    su = _mg(su, tk)
                    for ii in range(ni):
                        i = i0 + ii
                        t = scr_touch(act(gv[:, i, :], ps[:, 2 * p + ii, :], AF.Gelu_apprx_tanh, _mg(tk, dep_gv[i], stA_B.wr()),
                                          accum=s1t[:, i:i + 1]))
                        ps_rel(p, t)
                        def fsq(e, i=i):
                            return e.scalar_tensor_tensor(out=scr[:, 6144:6656], in0=gv[:, i, :], scalar=1.0, in1=gv[:, i, :],
                                                          op0=ALU.mult, op1=ALU.mult, accum_out=s2t[:, i:i + 1])
                        t = scr_touch(S.op("dve", fsq, _mg(t, dep_junk, junk_tok)))
                        junk_tok = t
                        sttok = _mg(sttok, t)
                hTB.read(su)
                ring_rel(s0, su)
                ring_rel(s1, su)

                guB = Buf()
                yw = yTB.wr()
                ytoks = {}
                uz = {}

                def uz_jobs(fc):
                    s, _ = ring_next()
                    p, tk = fm_job(s, 0, 8, hT, hd, coff=co)
                    t = act(bv(gu[:, 0:T]), psv(p), AF.Gelu_apprx_tanh, _mg(tk, guB.wr(), sigB[0].wr()))
                    ps_rel(p, t)
                    p2, tk2 = fm_job(s, 1024, 8, hT, hd, coff=co)
                    hTB.read(tk2)
                    ring_rel(s, tk2)
                    t2 = act(bv(sz[:, 0:T]), psv(p2), AF.Silu, _mg(tk2, guB.wr(), sigB[1].wr()))
                    ps_rel(p2, t2)
                    uz[fc] = (t, t2)

                t1 = ts(mean_t[:, sk:NT], s1t[:, sk:NT], 1.0 / 512, None, ALU.mult, None, sttok)
                t1 = tt(m2t[:, sk:NT], mean_t[:, sk:NT], mean_t[:, sk:NT], ALU.mult, t1)
                t1 = stt(var_t[:, sk:NT], s2t[:, sk:NT], 1.0 / 512, m2t[:, sk:NT], ALU.mult, ALU.subtract, t1)
                t1 = rsqrt(rl3[:, sk:NT], var_t[:, sk:NT], rl[:, sk:NT], rl2[:, sk:NT], 1.0, LN_EPS, t1)
                t1 = stt(nmr[:, sk:NT], mean_t[:, sk:NT], -1.0, rl3[:, sk:NT], ALU.mult, ALU.mult, t1)
                stA_B.wrote(t1)
                uz_jobs(0)
                vtok = {}
                for i in range(sk, NT):
                    t2 = scr_touch(ts(gv[:, i, :], gv[:, i, :], rl3[:, i:i + 1], nmr[:, i:i + 1], ALU.mult, ALU.add, t1))
                    t2 = scr_touch(tt(vnr[:, i, :], gv[:, i, :], lng, ALU.mult, _mg(t2, cld, dep_R)))
                    stA_B.read(t2)
                    vtok[i] = t2
                satok = {}
                for i in range(sk, NT):
                    p, pd = ps_alloc()
                    tk = None
                    for fc in range(4):
                        tk = mm(ps[:, 2 * p + fc // 2, (fc % 2) * 256:(fc % 2 + 1) * 256], vnr[:, i, fc * 128:(fc + 1) * 128],
                                wm[:, fc * 256:(fc + 1) * 256], True, True, _mg(pd, vtok[i], wmd), ms=True)
                    scr_touch(tk)
                    et = {}
                    for hf in range(2):
                        sl = slice(hf * 64, hf * 64 + 64)
                        src = ps[sl, 2 * p:2 * p + 2, :].rearrange("p b (g c) -> p (b g) c", g=2)[:, :, hf * 128:(hf + 1) * 128]
                        t2 = tt(saT[sl, :, (i - sk) * 128:(i - sk + 1) * 128], src, bias2[sl, :, :], ALU.add, _mg(tk, dep_saT, wmd))
                        et = _mg(et, t2)
                    scr_touch(et)
                    ps_rel(p, et)
                    satok = _mg(satok, et)
                for fc in range(4):
                    if fc > 0:
                        uz_jobs(fc)
                    t, t2 = uz[fc]
                    t3 = scr_touch(tt(gu[:, 0:T], gu[:, 0:T], saT[:, fc, 0:T], ALU.mult, _mg(t, satok)))
                    t4 = scr_touch(tt(yT[:, fc, 0:T], gu[:, 0:T], sz[:, 0:T], ALU.mult, _mg(t3, t2, yw)))
                    guB.wrote(t4)
                    sigB[0].wrote(t4)
                    sigB[1].wrote(t4)
                    ytoks = _mg(ytoks, t4)
                yTB.wrote(ytoks)
                if l == 0 and j == 0:
                    dump(1, yT[:].bitcast(F32).rearrange("p a b -> p (a b)"), 3072, ytoks, [yTB])
                outproj("a", True)
                if l == 0 and j == 0:
                    dump(2, mT[:].bitcast(F32).rearrange("p a b -> p (a b)"), 6144, _mg(*[b_.rd() for b_ in mTB]), mTB)

                scr_phase()
                sp = scr_prev[0]
                yw = yTB.wr()
                ytoks = {}
                bbuf = [Buf(), Buf()]
                for fc in range(4):
                    bs_ = fc % 2
                    base = bs_ * 3584
                    tb0 = scr[:, base:base + TX]
                    yy = scr[:, base + 768:base + 768 + TX + 2]
                    acc = scr[:, base + 1544:base + 1544 + T]
                    szb = scr[:, base + 2312:base + 2312 + T]
                    wdep = _mg(sp, bbuf[bs_].wr())
                    sA, _ = ring_next()
                    p, tk = fm_job(sA, 0, 8, hT, hd, coff=0, Tc=TX)
                    t = scr_touch(act(bv(tb0, TX), psv(p, TX), AF.Copy, _mg(tk, wdep)))
                    ps_rel(p, t)
                    p, tk = fm_job(sA, 1024, 8, hT, hd, coff=0, Tc=TX)
                    ring_rel(sA, tk)
                    tc = scr_touch(cp(yy[:, 0:2], yyc[:, fc, :], _mg(wdep, yycB.rd())))
                    t = scr_touch(tt(bv(yy[:, 2:2 + TX], TX), psv(p, TX), bv(tb0, TX), ALU.mult, _mg(tk, t, wdep)))
                    ps_rel(p, t)
                    t = _mg(t, tc)
                    tcar = cp(yyc[:, fc, :], yy[:, TX:TX + 2], _mg(t, yycB.wr()))
                    yycB.wrote(tcar, fresh=False)
                    t1 = scr_touch(ts(acc, yy[:, 2 + co:2 + co + T], cw[:, fc, 2:3], cb[:, fc:fc + 1], ALU.mult, ALU.add, _mg(t, cld, wdep)))
                    t1 = scr_touch(stt(acc, yy[:, 1 + co:1 + co + T], cw[:, fc, 1:2], acc, ALU.mult, ALU.add, t1))
                    t1 = scr_touch(stt(acc, yy[:, co:co + T], cw[:, fc, 0:1], acc, ALU.mult, ALU.add, t1))
                    sB, _ = ring_next()
                    p, tk = fm_job(sB, 0, 8, hT, hd, coff=co)
                    t1 = scr_touch(tt(bv(acc), psv(p), bv(acc), ALU.mult, _mg(tk, t1)))
                    ps_rel(p, t1)
                    p, tk = fm_job(sB, 1024, 8, hT, hd, coff=co)
                    hTB.read(tk)
                    ring_rel(sB, tk)
                    t2 = scr_touch(act(bv(szb), psv(p), AF.Silu, _mg(tk, wdep)))
                    ps_rel(p, t2)
                    t3 = scr_touch(tt(yT[:, fc, 0:T], acc, szb, ALU.mult, _mg(t1, t2, yw, tcar)))
                    bbuf[bs_].wrote(t3)
                    ytoks = _mg(ytoks, t3)
                yTB.wrote(ytoks)
                if l == 0 and j == 0:
                    dump(3, yT[:].bitcast(F32).rearrange("p a b -> p (a b)"), 3072, ytoks, [yTB])
                outproj("b", False)

                scr_phase()
                sp = scr_prev[0]
                pT = yT
                szc = [scr[:, 0:TM], scr[:, TM:2 * TM]]
                ywC = yTB.wr()
                H = 16
                xa = scr[:, 1536:1536 + H + TM]
                xb_ = scr[:, 2320:2320 + H + TM]
                xraws = [scr[:, 3104:3104 + H + TM], scr[:, 3904:3904 + H + TM]]
                tmp16 = scr[:, 3888:3904]
                xrawB = [Buf(), Buf()]
                END = H + TX
                ptok_g = {}
                yw = yTB.wr()
                ytoks = {}
                szB = [Buf(), Buf()]
                sxs = [ring_next()[0], ring_next()[0]]
                szs = [ring_next()[0], ring_next()[0]]
                zct = {}
                chain = {"t": {}}

                def xc_stage(gi):
                    w = (2, 4, 8, 16)[gi]
                    xraw = xraws[gi % 2]
                    sx = sxs[gi // 2]
                    p, tk = fm_job(sx, (gi % 2) * 1024, 8, hT, hd, coff=0, Tc=TX)
                    hTB.read(tk)
                    if gi % 2 == 1:
                        ring_rel(sx, tk)
                    tcy = scr_touch(cp(xraw[:, 0:H], xcc[:, gi, :], _mg(sp, xrawB[gi % 2].wr(), xccB.rd())))
                    t = scr_touch(act(bv(xraw[:, H:END], TX), psv(p, TX), AF.Copy, _mg(tk, sp, xrawB[gi % 2].wr())))
                    ps_rel(p, t)
                    t = _mg(t, tcy)
                    tcar = cp(xcc[:, gi, :], xraw[:, TX:TX + H], _mg(t, xccB.wr()))
                    xccB.wrote(tcar, fresh=False)
                    src = xraw
                    cur = _mg(t, tcar, chain["t"])
                    sh = 1
                    k_ = 0
                    while sh < w:
                        dst = (xa, xb_)[k_ % 2]
                        lo = 2 * sh - 1
                        cur = scr_touch(tt(dst[:, lo:END], src[:, lo:END], src[:, lo - sh:END - sh], ALU.add, _mg(cur, sp)))
                        src = dst
                        sh *= 2
                        k_ += 1
                    c0 = H + co
                    t2 = scr_touch(stt(pT[:, gi, 0:T], src[:, c0:c0 + T], 1.0 / w, xraw[:, c0:c0 + T], ALU.mult, ALU.subtract,
                                       _mg(cur, ywC)))
                    if t0 == 0:
                        cs = H + 128
                        pc = (1 - sk) * 128
                        t3 = scr_touch(tt(tmp16, src[:, cs:cs + H], invc[:, gi, :], ALU.mult, _mg(t2, cst_g)))
                        t2 = scr_touch(tt(pT[:, gi, pc:pc + H], tmp16, xraw[:, cs:cs + H], ALU.subtract, t3))
                    xrawB[gi % 2].wrote(t2)
                    chain["t"] = t2
                    ptok_g[gi] = t2

                def zc_stage(gi):
                    sz_s = szs[gi // 2]
                    p, tk = fm_job(sz_s, (gi % 2) * 1024, 8, hT, hd, coff=co)
                    hTB.read(tk)
                    if gi % 2 == 1:
                        ring_rel(sz_s, tk)
                    t = scr_touch(act(bv(szc[gi % 2][:, 0:T]), psv(p), AF.Silu, _mg(tk, sp, szB[gi % 2].wr())))
                    ps_rel(p, t)
                    zct[gi] = t

                def wp_stage(gi):
                    p, pd = ps_alloc()
                    tk = None
                    for ci, (c0, c1) in enumerate([(0, cwid), (cwid, T)]):
                        tk = mm(ps[:, 2 * p + ci, 0:cwid], wpool[:, gi * 128:(gi + 1) * 128], pT[:, gi, c0:c1], True, True,
                                _mg(pd, ptok_g[gi], cst_g), ms=True)
                    scr_touch(tk)
                    t2 = scr_touch(stt(bv(yT[:, gi, 0:T]), psv(p), psc[:, gi:gi + 1], bv(szc[gi % 2][:, 0:T]), ALU.mult, ALU.mult,
                                       _mg(tk, zct[gi], yw, cld)))
                    ps_rel(p, t2)
                    szB[gi % 2].wrote(t2)
                    return t2

                xc_stage(0)
                xc_stage(1)
                for gi in range(4):
                    zc_stage(gi)
                    ytoks = _mg(ytoks, wp_stage(gi))
                    if gi + 2 < 4:
                        xc_stage(gi + 2)
                yTB.wrote(ytoks)
                if l == 0 and j == 0:
                    dump(4, yT[:].bitcast(F32).rearrange("p a b -> p (a b)"), 3072, ytoks, [yTB])
                nxt = passes[n + 1] if n + 1 < len(passes) else None
                need_setup = nxt is not None and nxt[0] != l
                if need_setup:
                    setup_A(nxt[0], all_now())
                outproj("c", False)
                if need_setup:
                    setup_B(nxt[0])
                if l == 0 and j == 0:
                    dump(5, mT[:].bitcast(F32).rearrange("p a b -> p (a b)"), 6144, _mg(*[b_.rd() for b_ in mTB]), mTB)

                scr_phase()
                sp = scr_prev[0]
                xo = v3(scr[:, 0:NTM * 1024], 1024)
                fgb = scr[:, 6144:7168]
                md = _mg(*[b_.rd() for b_ in mTB])
                p0_list = []
                if nxt is not None:
                    nNT = SUPER[nxt[1]]
                    nt0 = sum(SUPER[:nxt[1]])
                    p0_list = [(nxt[0], nt0 + i, i) for i in range(nNT)]
                hdeps_n = hTB.wr()
                hw = {}
                stA = phase0_A(*p0_list.pop(0)) if p0_list else None
                if l == DEPTH - 1:
                    tfg = scr_touch(dma("pool", fgb, fg_d, "fgl", sp))
                xo_tok = {}
                x1_tok = {}
                xo_reg = [[{}, {}] for _ in range(NTM)]
                ctx["sp_C"] = sp
                fg_rd = dict(tfg) if l == DEPTH - 1 else {}
                add_tok = {}
                ctx["fg_rd"] = fg_rd
                pend = []

                def fin_a(i, dep):
                    junk = sig[:].rearrange("p a b -> p (a b)")[:, 0:D]
                    t = scr_touch(act(junk, xo[:, i, :], AF.Square, _mg(dep, sigB[0].wr(), sigB[1].wr(), stat2B[i].wr()),
                                      accum=ssq2[:, i:i + 1]))
                    sigB[0].w = _mg(sigB[0].w, t)
                    sigB[1].w = _mg(sigB[1].w, t)
                    t = rsqrt(rst2[:, i:i + 1], ssq2[:, i:i + 1], tmpa2[:, i:i + 1], tmpb2[:, i:i + 1], 1.0 / D, RMS_EPS, t)
                    pend.append((i, t))

                def fin_b(i, t):
                    t = scr_touch(stt(xo[:, i, :], xo[:, i, :], rst2[:, i:i + 1], fgb, ALU.mult, ALU.mult, _mg(t, tfg)))
                    stat2B[i].wrote(t)
                    ctx["fg_rd"] = _mg(ctx["fg_rd"], t)
                    if t0 + i >= 1:
                        r0 = (t0 + i - 1) * 128
                        t = scr_touch(dma("sp", out_d[r0:r0 + 128, :], xo[:, i, :], "so%d" % i, t))
                        ctx["out_toks"] = _mg(ctx.get("out_toks"), t)
                    xo_reg[i][0] = xo_reg[i][1] = t

                for half in range(2):
                    sa, _ = ring_next()
                    sb2, _ = ring_next()
                    src = xsrc[(t0 + sk) * 128:(t0 + NT) * 128, half * 512:(half + 1) * 512].rearrange("(i p) d -> p i d", p=128)
                    tx = scr_touch(dma("pool", xo[:, sk:NT, half * 512:(half + 1) * 512], src, "xin%d" % half,
                                       _mg(sp, x1_st.get((l - 1, j)) if l > 0 else None)))
                    su = {}
                    for i0 in range(sk, NT, 2):
                        ni = min(2, NT - i0)
                        p, pd = ps_alloc()
                        tk = None
                        for ii in range(ni):
                            i = i0 + ii
                            for h, s in enumerate((sa, sb2)):
                                for k in range(8):
                                    tk = mm(ps[:, 2 * p + ii, h * 256:(h + 1) * 256], mT[:, k, (i - sk) * 128:(i - sk + 1) * 128],
                                            ring[:, s, k * 256:(k + 1) * 256], k == 0, k == 7,
                                            _mg(pd, md, slot_tok[sa], slot_tok[sb2]) if k == 0 else None, ms=(k == 7))
                        su = _mg(su, tk)
                        do_b = stA is not None
                        if do_b:
                            nstA = phase0_A(*p0_list.pop(0)) if p0_list else None
                        t = scr_touch(tt(xo[:, i0:i0 + ni, half * 512:(half + 1) * 512], ps[:, 2 * p:2 * p + ni, :],
                                         xo[:, i0:i0 + ni, half * 512:(half + 1) * 512], ALU.add, _mg(tk, tx, sp)))
                        ps_rel(p, t)
                        xo_tok = _mg(xo_tok, t)
                        add_tok[(half, i0)] = t
                        if do_b:
                            hw = _mg(hw, phase0_B(stA, hdeps_n))
                            stA = nstA
                        if l == DEPTH - 1 and half == 1:
                            while len(pend) > 0:
                                fin_b(*pend.pop(0))
                            for i_ in range(i0, i0 + ni):
                                fin_a(i_, _mg(t, add_tok[(0, i0)]))
                        if l < DEPTH - 1:
                            dst = x1_d[(t0 + i0) * 128:(t0 + i0 + ni) * 128, half * 512:(half + 1) * 512].rearrange(
                                "(i p) d -> p i d", p=128)
                            t_ = scr_touch(dma("sp", dst, xo[:, i0:i0 + ni, half * 512:(half + 1) * 512], "sx%d" % (half * 3 + i0 // 2), t))
                            x1_tok = _mg(x1_tok, t_)
                            for i_ in range(i0, i0 + ni):
                                xo_reg[i_][half] = t_
                    for b_ in mTB:
                        b_.read(su)
                    ring_rel(sa, su)
                    ring_rel(sb2, su)
                while stA is not None:
                    nstA = phase0_A(*p0_list.pop(0)) if p0_list else None
                    hw = _mg(hw, phase0_B(stA, hdeps_n))
                    stA = nstA
                if nxt is not None:
                    hTB.wrote(hw)
                if l == 0 and j == 0:
                    dump(6, scr[:, 0:6144], 6144, xo_tok)
                ctx["xo_reg"] = xo_reg
                if l < DEPTH - 1:
                    x1_st[(l, j)] = x1_tok
                else:
                    while pend:
                        fin_b(*pend.pop(0))


        S.op("pool", lambda e: e.memset(tmp8[:, 0:1], 0.0), _mg(ctx.get("out_toks"), dbg_toks), ms=False)

        with nc.Block() as block:
            @block.sync
            def _(e):
                S.run("sp", e, sems)

            @block.tensor
            def _(e):
                S.run("pe", e, sems)

            @block.scalar
            def _(e):
                S.run("act", e, sems)

            @block.vector
            def _(e):
                S.run("dve", e, sems)

            @block.gpsimd
            def _(e):
                S.run("pool", e, sems)
    return nc


def _fm_unit(W, c0):
    K = W.shape[0]
    return W[:, c0:c0 + 128].reshape(K // 128, 128, 128).transpose(1, 0, 2).reshape(128, -1)


def _tm_unit(W, c0):
    K = W.shape[0]
    return W[:, c0:c0 + 256].reshape(K // 128, 128, 256).transpose(1, 0, 2).reshape(128, -1)


def _weight_stream(w_in, w_pa, w_pb, w_pc, w_o):
    slots = []

    def outproj(Wp, cg0):
        def g(i):
            return np.concatenate([_fm_unit(w_in, cg0 + (2 * i) * 128), _fm_unit(w_in, cg0 + (2 * i + 1) * 128)], axis=1)

        def wp(i):
            return np.concatenate([_fm_unit(Wp, (4 * i + q) * 128) for q in range(4)], axis=1)
        return [g(0), wp(0), g(1), g(2), wp(1), g(3)]

    slots += [_tm_unit(w_in, C_V), _tm_unit(w_in, C_V + 256)]
    for fc in range(4):
        slots.append(np.concatenate([_fm_unit(w_in, C_U + fc * 128), _fm_unit(w_in, C_ZA + fc * 128)], axis=1))
    slots += outproj(w_pa, C_GA)
    for fc in range(4):
        slots.append(np.concatenate([_fm_unit(w_in, C_XB + fc * 128), _fm_unit(w_in, C_CG + fc * 128)], axis=1))
        slots.append(np.concatenate([_fm_unit(w_in, C_BG + fc * 128), _fm_unit(w_in, C_ZB + fc * 128)], axis=1))
    slots += outproj(w_pb, C_GB)
    slots.append(np.concatenate([_fm_unit(w_in, C_XC), _fm_unit(w_in, C_XC + 128)], axis=1))
    slots.append(np.concatenate([_fm_unit(w_in, C_XC + 256), _fm_unit(w_in, C_XC + 384)], axis=1))
    slots.append(np.concatenate([_fm_unit(w_in, C_ZC), _fm_unit(w_in, C_ZC + 128)], axis=1))
    slots.append(np.concatenate([_fm_unit(w_in, C_ZC + 256), _fm_unit(w_in, C_ZC + 384)], axis=1))
    slots += outproj(w_pc, C_GC)
    for cb in range(4):
        slots.append(_tm_unit(w_o, cb * 256))
    assert len(slots) == NSLOT_L
    return np.stack(slots, axis=0)


def _band_consts():
    s = np.arange(128)[:, None]
    t = np.arange(128)[None, :]
    band = np.zeros((128, 4, 128), np.float32)
    bandp = np.zeros((128, 4, 128), np.float32)
    bandf = np.zeros((128, 4, 128), np.float32)
    for gi, w in enumerate((2, 4, 8, 16)):
        inw = (s <= t) & (s > t - w)
        band[:, gi, :] = np.where(inw, np.float32(1.0 / w), 0.0) - np.where(s == t, 1.0, 0.0)
        inwp = ((s - 128) > t - w)
        bandp[:, gi, :] = np.where(inwp, np.float32(1.0 / w), 0.0)
        cnt = np.minimum(t + 1, w).astype(np.float32)
        bandf[:, gi, :] = np.where(inw, np.float32(1.0) / cnt, 0.0) - np.where(s == t, 1.0, 0.0)
    return band.astype(np.float32), bandp.astype(np.float32), bandf.astype(np.float32)


_NC_CACHE = {}


def kernel(x, norm_g, w_in, ln_g, ln_b, w_s, b_s, conv_w, conv_b, w_pool, pool_scale,
           w_pa, w_pb, w_pc, w_o, final_g):
    f = lambda a: np.ascontiguousarray(np.asarray(a, dtype=np.float32))
    x, norm_g, w_in, ln_g, ln_b, w_s, b_s = map(f, (x, norm_g, w_in, ln_g, ln_b, w_s, b_s))
    conv_w, conv_b, w_pool, pool_scale = map(f, (conv_w, conv_b, w_pool, pool_scale))
    w_pa, w_pb, w_pc, w_o, final_g = map(f, (w_pa, w_pb, w_pc, w_o, final_g))
    B, SEQ, _ = x.shape
    cores_per_b = NCORES // B

    ws = np.concatenate([_weight_stream(w_in[l], w_pa[l], w_pb[l], w_pc[l], w_o[l]) for l in range(DEPTH)], axis=0)
    ws = np.ascontiguousarray(ws, dtype=np.float32)

    cl = np.zeros((DEPTH, 128, CL), np.float32)
    for l in range(DEPTH):
        cl[l, :, CL_G:CL_G + 8] = norm_g[l].reshape(8, 128).T
        cl[l, :, CL_LNB:CL_LNB + 4] = ln_b[l].reshape(4, 128).T
        cl[l, :, CL_CW:CL_CW + 12] = conv_w[l].reshape(3, 4, 128).transpose(2, 1, 0).reshape(128, 12)
        cl[l, :, CL_CB:CL_CB + 4] = conv_b[l].reshape(4, 128).T
        cl[l, :, CL_PS:CL_PS + 4] = pool_scale[l].reshape(4, 128).T
        cl[l, :, CL_LNG:CL_LNG + 4] = ln_g[l].reshape(4, 128).T
        cl[l, :, CL_LNGB:CL_LNGB + 512] = np.broadcast_to(ln_g[l][None, :], (128, 512))
        bs = b_s[l].reshape(4, 2, 1, 128)
        cl[l, :, CL_BS:CL_BS + 512] = np.broadcast_to(bs, (4, 2, 64, 128)).transpose(1, 2, 0, 3).reshape(128, 512)
        cl[l, :, CL_WS:CL_WS + 1024] = w_s[l].transpose(2, 0, 1).reshape(128, 1024)
    band, bandp, bandf = _band_consts()
    ones = np.ones((128, 128), np.float32)
    def cg_for(first):
        inv = np.zeros((4, 16), np.float32)
        for gi_, w_ in enumerate((2, 4, 8, 16)):
            cnt_ = np.minimum(np.arange(1, 17), w_) if first else np.full(16, w_)
            inv[gi_] = (np.float32(1.0) / cnt_.astype(np.float32)).astype(np.float32)
        return np.ascontiguousarray(np.concatenate(
            [np.eye(128, dtype=np.float32), np.triu(np.ones((128, 128), np.float32)),
             np.broadcast_to(inv.reshape(1, 64), (128, 64))], axis=1), dtype=np.float32)

    cg_first, cg_other = cg_for(True), cg_for(False)
    fg = np.ascontiguousarray(np.broadcast_to(final_g[None, :], (128, D)), dtype=np.float32)

    def cr_for(first):
        parts = [w_pool[l].transpose(1, 0, 2).reshape(128, 512) for l in range(DEPTH)]
        parts += [band.reshape(128, 512), bandp.reshape(128, 512),
                  (bandf if first else band).reshape(128, 512), ones]
        return np.ascontiguousarray(np.concatenate(parts, axis=1), dtype=np.float32)

    cr_first, cr_other = cr_for(True), cr_for(False)

    in_maps = []
    for c in range(NCORES):
        b = c // cores_per_b
        s0 = (c % cores_per_b) * TOK_CORE
        xc = np.zeros((NTILES * 128, D), np.float32)
        if s0 == 0:
            xc[128:] = x[b, 0:TOK_CORE]
        else:
            xc[:] = x[b, s0 - 128:s0 + TOK_CORE]
        in_maps.append({"x": xc, "ws": ws, "cl": cl, "cr": cr_first if s0 == 0 else cr_other, "cg": cg_first if s0 == 0 else cg_other, "fg": fg})

    if "nc" not in _NC_CACHE:
        _NC_CACHE["nc"] = build_program()
    nc = _NC_CACHE["nc"]
    res = run_bass_kernel_spmd(nc, in_maps, core_ids=list(range(NCORES)))
    if _DBG:
        _NC_CACHE["dbg"] = [res.results[c]["dbg"] for c in range(NCORES)]
    out = np.empty((B, SEQ, D), np.float32)
    for c in range(NCORES):
        b = c // cores_per_b
        s0 = (c % cores_per_b) * TOK_CORE
        out[b, s0:s0 + TOK_CORE] = res.results[c]["out"]
    return out
```

```python
import numpy as np
from contextlib import ExitStack
import concourse.bass as bass
import concourse.mybir as mybir
from concourse.bass_utils import run_bass_kernel_spmd

F32 = mybir.dt.float32
F32R = mybir.dt.float32r
AF = mybir.ActivationFunctionType
ALU = mybir.AluOpType

D = 1024
NCORES = 8
TOK_CORE = 2048
NTILES = 17
SUPER = [6, 6, 5]
NTM = 6
TM = NTM * 128
DEPTH = 2
HALO_SKIP = True
_DBG = False
SLOT = 2048
NSLOT_L = 40
RING = 6
RMS_EPS = 1e-6
LN_EPS = 1e-5

C_U, C_V, C_ZA, C_XB, C_BG, C_CG, C_ZB, C_XC, C_ZC, C_GA, C_GB, C_GC = (
    0, 512, 1024, 1536, 2048, 2560, 3072, 3584, 4096, 4608, 5632, 6656)

CL_G, CL_LNB, CL_CW, CL_CB, CL_PS = 0, 8, 12, 24, 28
CL_LNG, CL_BS, CL_WS = 32, 36, 548
CL_LNGB = 1572
CL = 2084
CR_L = 512
CGW = 320


def _mg(*ds):
    r = {}
    for d in ds:
        if not d:
            continue
        for k, v in d.items():
            if r.get(k, 0) < v:
                r[k] = v
    return r


class Sched:
    def __init__(self):
        self.ops = {}
        self.cnt = {}

    def op(self, eng, fn, deps=None, sem=None, amt=1, ms=True):
        tok = None
        semname = None
        if ms:
            semname = sem or eng
            self.cnt[semname] = self.cnt.get(semname, 0) + amt
            tok = {semname: self.cnt[semname]}
        self.ops.setdefault(eng, []).append((deps or {}, fn, semname, amt))
        return tok

    def run(self, eng_name, eng, sems):
        known = {}
        for deps, fn, semname, amt in self.ops.get(eng_name, []):
            for s, v in deps.items():
                if known.get(s, 0) >= v:
                    continue
                eng.wait_ge(sems[s], v)
                known[s] = v
            ins = fn(eng)
            if semname is not None:
                ins.then_inc(sems[semname], amt)


class Buf:
    def __init__(self):
        self.w = {}
        self.r = {}

    def rd(self):
        return self.w

    def wr(self):
        return _mg(self.w, self.r)

    def wrote(self, tok, fresh=True):
        if fresh:
            self.w = dict(tok)
            self.r = {}
        else:
            self.w = _mg(self.w, tok)

    def read(self, tok):
        self.r = _mg(self.r, tok)


def build_program():
    nc = bass.Bass("TRN2", target_bir_lowering=False)
    nc.dge_precook = False
    x_d = nc.dram_tensor("x", [NTILES * 128, D], F32, kind="ExternalInput").ap()
    ws_d = nc.dram_tensor("ws", [DEPTH * NSLOT_L, 128, SLOT], F32, kind="ExternalInput").ap()
    cl_d = nc.dram_tensor("cl", [DEPTH, 128, CL], F32, kind="ExternalInput").ap()
    cr_d = nc.dram_tensor("cr", [128, DEPTH * CR_L + 1664], F32, kind="ExternalInput").ap()
    cg_d = nc.dram_tensor("cg", [128, CGW], F32, kind="ExternalInput").ap()
    fg_d = nc.dram_tensor("fg", [128, D], F32, kind="ExternalInput").ap()
    out_d = nc.dram_tensor("out", [TOK_CORE, D], F32, kind="ExternalOutput").ap()
    x1_d = nc.dram_tensor("x1", [NTILES * 128, D], F32).ap()
    dbg_d = nc.dram_tensor("dbg", [8, 128, 6144], F32, kind="ExternalOutput").ap() if _DBG else None

    S = Sched()
    es = ExitStack()

    def sb(name, shape, dt):
        return es.enter_context(nc.sbuf_tensor(name, shape, dt))

    with es:
        hT = sb("hT", [128, 8, TM], F32R)
        mT = sb("mT", [128, 8, TM], F32R)
        yT = sb("yT", [128, 4, TM], F32R)
        ring = sb("ring", [128, RING, SLOT], F32R)
        scr = sb("scr", [128, 7168], F32)
        scrR = sb("scrR", [128, 3584], F32R)
        xt = sb("xt", [128, 2, D], F32)
        xs = sb("xs", [128, 2, D], F32)
        sig = sb("sig", [128, 2, TM], F32)
        clt = sb("clt", [128, CL], F32)
        crt = sb("crt", [128, DEPTH * CR_L + 1664], F32R)
        cgt = sb("cgt", [128, CGW], F32)
        wm = sb("wm", [128, 1024], F32R)
        bias2 = sb("bias2", [128, 4, 128], F32)
        ssq = sb("ssq", [128, 8], F32)
        rst = sb("rst", [128, 8], F32)
        tmp8 = sb("tmp8", [128, 8], F32)
        bnst = sb("bnst", [128, NTM, 6], F32)
        mv = sb("mv", [128, NTM, 2], F32)
        rl = sb("rl", [128, NTM], F32)
        rl2 = sb("rl2", [128, NTM], F32)
        rl3 = sb("rl3", [128, NTM], F32)
        nmr = sb("nmr", [128, NTM], F32)
        s1t = sb("s1t", [128, NTM], F32)
        s2t = sb("s2t", [128, NTM], F32)
        mean_t = sb("mean_t", [128, NTM], F32)
        m2t = sb("m2t", [128, NTM], F32)
        var_t = sb("var_t", [128, NTM], F32)
        tmp9 = sb("tmp9", [128, 8], F32)
        yyc = sb("yyc", [128, 4, 2], F32)
        xcc = sb("xcc", [128, 4, 16], F32)
        ps = es.enter_context(nc.psum_tensor("ps", [128, 8, 512], F32))

        dbg_toks = {}
        sem_names = ["dbgs", "pe", "act", "dve", "pool", "cg", "cr", "cl", "xl0", "xl1", "xin0", "xin1", "sx0", "sx1", "sx2", "sx3", "sx4", "sx5", "so0", "so1", "so2", "so3", "so4", "so5", "sto", "fgl"] + \
            ["w%d" % s for s in range(RING)]
        sems = {n: es.enter_context(nc.semaphore(n)) for n in sem_names}

        def mm(out, lhsT, rhs, start, stop, deps=None, ms=False):
            return S.op("pe", lambda e: e.matmul(out, lhsT, rhs, start=start, stop=stop), deps, ms=ms)

        def tr(out, in_, ident, deps=None, ms=False):
            return S.op("pe", lambda e: e.transpose(out, in_, ident), deps, ms=ms)

        def act(out, in_, func, deps=None, scale=None, accum=None):
            def f(e):
                kw = {}
                if scale is not None:
                    kw["scale"] = scale
                if accum is not None:
                    kw["accum_out"] = accum
                return e.activation(out=out, in_=in_, func=func, **kw)
            return S.op("act", f, deps)

        def ts(out, in0, s1, s2, op0, op1=None, deps=None):
            def f(e):
                if op1 is None:
                    return e.tensor_scalar(out=out, in0=in0, scalar1=s1, scalar2=None, op0=op0)
                return e.tensor_scalar(out=out, in0=in0, scalar1=s1, scalar2=s2, op0=op0, op1=op1)
            return S.op("dve", f, deps)

        def tt(out, in0, in1, op, deps=None):
            return S.op("dve", lambda e: e.tensor_tensor(out=out, in0=in0, in1=in1, op=op), deps)

        def stt(out, in0, sc, in1, op0, op1, deps=None):
            return S.op("dve", lambda e: e.scalar_tensor_tensor(out=out, in0=in0, scalar=sc, in1=in1, op0=op0, op1=op1), deps)

        def cp(out, in_, deps=None):
            return S.op("dve", lambda e: e.tensor_copy(out=out, in_=in_), deps)

        def rsqrt(dst, src, ta, tb, scale, eps, deps):
            t = ts(ta, src, scale, eps, ALU.mult, ALU.add, deps)
            t = act(tb, ta, AF.Sqrt, t)
            return S.op("dve", lambda e: e.reciprocal(out=dst, in_=tb), t)

        def dma(q, out, in_, sem, deps=None):
            return S.op(q, lambda e: e.dma_start(out=out, in_=in_), deps, sem=sem, amt=16)

        def v3(ap, b):
            return ap.rearrange("p (a b) -> p a b", b=b)

        def dump(idx, ap, width, deps, bufs=()):
            if not _DBG:
                return
            t = dma("pool", dbg_d[idx, :, 0:width], ap, "dbgs", deps)
            for b_ in bufs:
                b_.read(t)
            scr_touch(t)
            dbg_toks.update(_mg(dbg_toks, t))

        ps_rd = [dict() for _ in range(4)]
        ps_nxt = [0]

        def ps_alloc():
            p = ps_nxt[0]
            ps_nxt[0] = (p + 1) % 4
            d = ps_rd[p]
            ps_rd[p] = {}
            return p, d

        def ps_rel(p, tok):
            ps_rd[p] = _mg(ps_rd[p], tok)

        total_items = DEPTH * len(SUPER) * NSLOT_L
        rg = {"q": 0, "c": 0}
        slot_rd = [dict() for _ in range(RING)]
        slot_tok = [None] * RING

        slot_free = [True] * RING

        def ring_issue():
            while rg["q"] < total_items and slot_free[rg["q"] % RING]:
                q = rg["q"]
                s = q % RING
                layer = q // (len(SUPER) * NSLOT_L)
                item = q % NSLOT_L
                src = ws_d[layer * NSLOT_L + item].bitcast(F32R)
                slot_tok[s] = dma("sp", ring[:, s, :], src, "w%d" % s, slot_rd[s])
                slot_rd[s] = {}
                slot_free[s] = False
                rg["q"] = q + 1

        def ring_next():
            c = rg["c"]
            rg["c"] = c + 1
            s = c % RING
            return s, slot_tok[s]

        def ring_rel(s, tok):
            slot_rd[s] = _mg(slot_rd[s], tok)
            slot_free[s] = True
            ring_issue()

        cst_g = _mg(dma("pool", cgt[:], cg_d, "cg"), dma("pool", crt[:], cr_d.bitcast(F32R), "cr"))
        ident = cgt[:, 0:128]
        maskT = cgt[:, 128:256]
        invc = v3(cgt[:, 256:320], 16)
        GB = DEPTH * CR_L
        band = crt[:, GB:GB + 512]
        bandp = crt[:, GB + 512:GB + 1024]
        bandf = crt[:, GB + 1024:GB + 1536]
        ones = crt[:, GB + 1536:GB + 1664]

        ring_issue()

        clB = Buf()
        wmB = Buf()
        hTB = Buf()
        mTB = [Buf() for _ in range(8)]
        yTB = Buf()
        xtB = [Buf(), Buf()]
        xsB = [Buf(), Buf()]
        sigB = [Buf(), Buf()]
        statB = [Buf() for _ in range(8)]
        yycB = Buf()
        xccB = Buf()
        scr_prev = [dict()]
        scr_cur = [dict()]
        x1_st = {}
        out_toks = {}

        def scr_touch(tok):
            scr_cur[0] = _mg(scr_cur[0], tok)
            return tok

        def scr_phase():
            scr_prev[0] = _mg(scr_prev[0], scr_cur[0])
            scr_cur[0] = {}

        xl_cnt = [0]


        ssq2 = sb("ssq2", [128, 8], F32)
        rst2 = sb("rst2", [128, 8], F32)
        tmpa2 = sb("tmpa2", [128, 8], F32)
        tmpb2 = sb("tmpb2", [128, 8], F32)
        stat2B = [Buf() for _ in range(8)]
        junkB = Buf()
        stA_B = Buf()
        g_fm = clt[:, CL_G:CL_G + 8]
        lnb = clt[:, CL_LNB:CL_LNB + 4]
        cw = v3(clt[:, CL_CW:CL_CW + 12], 3)
        cb = clt[:, CL_CB:CL_CB + 4]
        psc = clt[:, CL_PS:CL_PS + 4]
        lngf = clt[:, CL_LNG:CL_LNG + 4]
        lng = clt[:, CL_LNGB:CL_LNGB + 512]
        bsb = v3(clt[:, CL_BS:CL_BS + 512], 128)
        wsT = clt[:, CL_WS:CL_WS + 1024]
        ctx = {}

        def all_now():
            return {e_: S.cnt[e_] for e_ in ("pe", "act", "dve") if S.cnt.get(e_)}

        def setup_A(l, bar=None):
            bar = bar or {}
            t = dma("pool", clt[:], cl_d[l], "cl", bar)
            cld = _mg(t, cst_g)
            ctx["cld"] = cld
            ctx["bar"] = bar
            wtoks = {}
            for g in range(8):
                tk = tt(wm[:, g * 128:(g + 1) * 128], wsT[:, g * 128:(g + 1) * 128], maskT, ALU.mult, _mg(cld, bar))
                wtoks = _mg(wtoks, tk)
            ctx["wtoks"] = wtoks
            t = S.op("dve", lambda e: e.memset(yyc[:], 0.0), bar)
            yycB.wrote(t)
            t = S.op("dve", lambda e: e.memset(xcc[:], 0.0), bar)
            xccB.wrote(t)

        def setup_B(l):
            cld = ctx["cld"]
            bar = ctx["bar"]
            wtoks = ctx["wtoks"]
            p, pd = ps_alloc()
            tk = None
            for g in range(8):
                tk = mm(ps[:, 2 * p + g // 4, (g % 4) * 128:(g % 4 + 1) * 128], ones, wm[:, g * 128:(g + 1) * 128],
                        True, True, _mg(pd, wtoks), ms=True)
            btok = {}
            for fc in range(4):
                for hf in range(2):
                    g = 2 * fc + hf
                    sl = slice(hf * 64, hf * 64 + 64)
                    t2 = stt(bias2[sl, fc, :], ps[sl, 2 * p + g // 4, (g % 4) * 128:(g % 4 + 1) * 128],
                             lnb[sl, fc:fc + 1], bsb[sl, fc, :], ALU.mult, ALU.add, _mg(tk, cld, bar))
                    btok = _mg(btok, t2)
            ps_rel(p, btok)
            ctx["wmd"] = _mg(wtoks, btok)

        def st_of(gt):
            return [k for k in range(len(SUPER)) if sum(SUPER[:k]) <= gt < sum(SUPER[:k + 1])][0]

        def phase0_A(l, gt, i):
            xsrc = x_d if l == 0 else x1_d
            b = xl_cnt[0] % 2
            xl_cnt[0] += 1
            ld_dep = xtB[b].wr()
            if l > 0:
                ld_dep = _mg(ld_dep, x1_st[(l - 1, st_of(gt))])
            t = dma("pool", xt[:, b, :], xsrc[gt * 128:(gt + 1) * 128, :], "xl%d" % b, ld_dep)
            xtB[b].wrote(t)
            t = act(xs[:, b, :], xt[:, b, :], AF.Square, _mg(xtB[b].rd(), xsB[b].wr(), statB[i].wr()), accum=ssq[:, i:i + 1])
            xtB[b].read(t)
            t = rsqrt(rst[:, i:i + 1], ssq[:, i:i + 1], tmp8[:, i:i + 1], tmp9[:, i:i + 1], 1.0 / D, RMS_EPS, t)
            t = act(xs[:, b, :], xt[:, b, :], AF.Copy, t, scale=rst[:, i:i + 1])
            statB[i].wrote(t)
            xtB[b].read(t)
            xsB[b].wrote(t)
            return (b, i)

        def phase0_B(st, hdeps):
            b, i = st
            cld = ctx["cld"]
            p, pd = ps_alloc()
            tk = None
            for k in range(8):
                tk = tr(ps[:, 2 * p + k // 4, (k % 4) * 128:(k % 4 + 1) * 128], xs[:, b, k * 128:(k + 1) * 128], ident,
                        _mg(pd, xsB[b].rd(), cst_g) if k == 0 else None, ms=(k == 7))
            xsB[b].read(tk)
            et = {}
            for k0 in (0, 4):
                t2 = tt(hT[:, k0:k0 + 4, i * 128:(i + 1) * 128], v3(ps[:, 2 * p + k0 // 4, :], 128),
                        g_fm[:, k0:k0 + 4].rearrange("p (k o) -> p k o", o=1).to_broadcast([128, 4, 128]),
                        ALU.mult, _mg(tk, hdeps, cld))
                et = _mg(et, t2)
            ps_rel(p, et)
            ctx.setdefault("hT_tile", {})[i] = _mg(et, hdeps)
            return et

        passes = [(l, j) for l in range(DEPTH) for j in range(len(SUPER))]
        setup_A(0)
        setup_B(0)
        hw = {}
        hdeps0 = hTB.wr()
        st_ = phase0_A(0, 0, 0)
        for i in range(SUPER[0]):
            nst_ = phase0_A(0, i + 1, i + 1) if i + 1 < SUPER[0] else None
            hw = _mg(hw, phase0_B(st_, hdeps0))
            st_ = nst_
        hTB.wrote(hw)

        for n, (l, j) in enumerate(passes):
            if True:
                NT = SUPER[j]
                t0 = sum(SUPER[:j])
                xsrc = x_d if l == 0 else x1_d
                cld = ctx["cld"]
                wmd = ctx["wmd"]
                wpool = crt[:, l * CR_L:(l + 1) * CR_L]
                sk = 1 if (l > 0 and l == DEPTH - 1 and j == 0 and HALO_SKIP) else 0
                co = sk * 128
                TX = NT * 128
                T = (NT - sk) * 128
                cwid = T // 2
                hd = hTB.rd()
                if l == 0 and j == 0:
                    dump(0, hT[:].bitcast(F32).rearrange("p a b -> p (a b)"), 6144, hd, [hTB])

                def fm_job(s, off, kch, rhs3, rdeps, coff=0, Tc=None):
                    Tc = Tc or T
                    cw_ = Tc // 2
                    p, pd = ps_alloc()
                    tk = None
                    for ci, (c0, c1) in enumerate([(0, cw_), (cw_, Tc)]):
                        for k in range(kch):
                            tk = mm(ps[:, 2 * p + ci, 0:cw_], ring[:, s, off + k * 128:off + (k + 1) * 128],
                                    rhs3[:, k, coff + c0:coff + c1], k == 0, k == kch - 1,
                                    _mg(pd, rdeps, slot_tok[s]) if (k == 0) else None, ms=(k == kch - 1))
                    return p, tk

                def psv(p, Tc=None):
                    return ps[:, 2 * p:2 * p + 2, 0:(Tc or T) // 2]

                def bv(ap2, Tc=None):
                    return v3(ap2, (Tc or T) // 2)

                def outproj(br, first):
                    yd = yTB.rd()
                    order = ["g0", "wp0", "g1", "g2", "wp1", "g3"]
                    it = iter(order)
                    held = {}
                    used = {}

                    def get(name):
                        while name not in held:
                            nm = next(it)
                            s, _ = ring_next()
                            held[nm] = s
                            used[nm] = {}
                        return held[name]

                    for oc in range(8):
                        gs = get("g%d" % (oc // 2))
                        ws_ = get("wp%d" % (oc // 4))
                        p, tk = fm_job(gs, (oc % 2) * 1024, 8, hT, hd, coff=co)
                        used["g%d" % (oc // 2)] = _mg(used["g%d" % (oc // 2)], tk)
                        hTB.read(tk)
                        sb_ = oc % 2
                        t = act(bv(sig[:, sb_, 0:T]), psv(p), AF.Sigmoid, _mg(tk, sigB[sb_].wr()))
                        ps_rel(p, t)
                        sigB[sb_].wrote(t)
                        p, tk = fm_job(ws_, (oc % 4) * 512, 4, yT, yd)
                        used["wp%d" % (oc // 4)] = _mg(used["wp%d" % (oc // 4)], tk)
                        yTB.read(tk)
                        if first:
                            t = tt(bv(mT[:, oc, 0:T]), psv(p), bv(sig[:, sb_, 0:T]), ALU.mult,
                                   _mg(tk, sigB[sb_].rd(), mTB[oc].wr()))
                            ps_rel(p, t)
                            sigB[sb_].read(t)
                            mTB[oc].wrote(t)
                        else:
                            t = tt(bv(sig[:, sb_, 0:T]), psv(p), bv(sig[:, sb_, 0:T]), ALU.mult,
                                   _mg(tk, sigB[sb_].rd()))
                            ps_rel(p, t)
                            t = tt(mT[:, oc, 0:T], mT[:, oc, 0:T].bitcast(F32), sig[:, sb_, 0:T], ALU.add,
                                   _mg(t, mTB[oc].wr()))
                            sigB[sb_].read(t)
                            mTB[oc].wrote(t)
                        if oc % 2 == 1:
                            nm = "g%d" % (oc // 2)
                            ring_rel(held[nm], used[nm])
                        if oc % 4 == 3:
                            nm = "wp%d" % (oc // 4)
                            ring_rel(held[nm], used[nm])

                scr_phase()
                sp = scr_prev[0]
                if ctx.get("xo_reg") is not None:
                    xr = ctx["xo_reg"]
                    spC = ctx["sp_C"]
                    dep_gv = [_mg(spC, xr[i // 2][i % 2]) for i in range(NTM)]
                    dep_junk = _mg(spC, ctx["fg_rd"])
                    dep_saT = _mg(spC, *[xr[i][h] for i in range(3, NTM) for h in range(2)])
                    dep_R = spC
                else:
                    dep_gv = [sp] * NTM
                    dep_junk = dep_saT = dep_R = sp
                gv = v3(scr[:, 0:3072], 512)
                vnr = v3(scrR[:, 0:3072], 512)
                saT = v3(scr[:, 3072:3072 + 4 * TM], TM)
                gu = sig[:, 0, :]
                sz = sig[:, 1, :]
                s0, _ = ring_next()
                s1, _ = ring_next()
                su = {}
                sttok = {}
                junk_tok = {}
                for i0 in range(sk, NT, 2):
                    ni = min(2, NT - i0)
                    p, pd = ps_alloc()
                    tk = None
                    for ii in range(ni):
                        i = i0 + ii
                        for h, s in enumerate((s0, s1)):
                            for k in range(8):
                                tk = mm(ps[:, 2 * p + ii, h * 256:(h + 1) * 256], hT[:, k, i * 128:(i + 1) * 128],
                                        ring[:, s, k * 256:(k + 1) * 256], k == 0, k == 7,
                                        _mg(pd, *[ctx["hT_tile"][i_] for i_ in range(i0, i0 + ni)], slot_tok[s0], slot_tok[s1]) if k == 0 else None, ms=(k == 7))
                    su = _mg(su, tk)
                    for ii in range(ni):
                        i = i0 + ii
                        t = scr_touch(act(gv[:, i, :], ps[:, 2 * p + ii, :], AF.Gelu_apprx_tanh, _mg(tk, dep_gv[i], stA_B.wr()),
                                          accum=s1t[:, i:i + 1]))
                        ps_rel(p, t)
                        def fsq(e, i=i):
                            return e.scalar_tensor_tensor(out=scr[:, 6144:6656], in0=gv[:, i, :], scalar=1.0, in1=gv[:, i, :],
                                                          op0=ALU.mult, op1=ALU.mult, accum_out=s2t[:, i:i + 1])
                        t = scr_touch(S.op("dve", fsq, _mg(t, dep_junk, junk_tok)))
                        junk_tok = t
                        sttok = _mg(sttok, t)
                hTB.read(su)
                ring_rel(s0, su)
                ring_rel(s1, su)

                guB = Buf()
                yw = yTB.wr()
                ytoks = {}
                uz = {}

                def uz_jobs(fc):
                    s, _ = ring_next()
                    p, tk = fm_job(s, 0, 8, hT, hd, coff=co)
                    t = act(bv(gu[:, 0:T]), psv(p), AF.Gelu_apprx_tanh, _mg(tk, guB.wr(), sigB[0].wr()))
                    ps_rel(p, t)
                    p2, tk2 = fm_job(s, 1024, 8, hT, hd, coff=co)
                    hTB.read(tk2)
                    ring_rel(s, tk2)
                    t2 = act(bv(sz[:, 0:T]), psv(p2), AF.Silu, _mg(tk2, guB.wr(), sigB[1].wr()))
                    ps_rel(p2, t2)
                    uz[fc] = (t, t2)

                t1 = ts(mean_t[:, sk:NT], s1t[:, sk:NT], 1.0 / 512, None, ALU.mult, None, sttok)
                t1 = tt(m2t[:, sk:NT], mean_t[:, sk:NT], mean_t[:, sk:NT], ALU.mult, t1)
                t1 = stt(var_t[:, sk:NT], s2t[:, sk:NT], 1.0 / 512, m2t[:, sk:NT], ALU.mult, ALU.subtract, t1)
                t1 = rsqrt(rl3[:, sk:NT], var_t[:, sk:NT], rl[:, sk:NT], rl2[:, sk:NT], 1.0, LN_EPS, t1)
                t1 = stt(nmr[:, sk:NT], mean_t[:, sk:NT], -1.0, rl3[:, sk:NT], ALU.mult, ALU.mult, t1)
                stA_B.wrote(t1)
                uz_jobs(0)
                vtok = {}
                for i in range(sk, NT):
                    t2 = scr_touch(ts(gv[:, i, :], gv[:, i, :], rl3[:, i:i + 1], nmr[:, i:i + 1], ALU.mult, ALU.add, t1))
                    t2 = scr_touch(tt(vnr[:, i, :], gv[:, i, :], lng, ALU.mult, _mg(t2, cld, dep_R)))
                    stA_B.read(t2)
                    vtok[i] = t2
                satok = {}
                for i in range(sk, NT):
                    p, pd = ps_alloc()
                    tk = None
                    for fc in range(4):
                        tk = mm(ps[:, 2 * p + fc // 2, (fc % 2) * 256:(fc % 2 + 1) * 256], vnr[:, i, fc * 128:(fc + 1) * 128],
                                wm[:, fc * 256:(fc + 1) * 256], True, True, _mg(pd, vtok[i], wmd), ms=True)
                    scr_touch(tk)
                    et = {}
                    for hf in range(2):
                        sl = slice(hf * 64, hf * 64 + 64)
                        src = ps[sl, 2 * p:2 * p + 2, :].rearrange("p b (g c) -> p (b g) c", g=2)[:, :, hf * 128:(hf + 1) * 128]
                        t2 = tt(saT[sl, :, (i - sk) * 128:(i - sk + 1) * 128], src, bias2[sl, :, :], ALU.add, _mg(tk, dep_saT, wmd))
                        et = _mg(et, t2)
                    scr_touch(et)
                    ps_rel(p, et)
                    satok = _mg(satok, et)
                for fc in range(4):
                    if fc > 0:
                        uz_jobs(fc)
                    t, t2 = uz[fc]
                    t3 = scr_touch(tt(gu[:, 0:T], gu[:, 0:T], saT[:, fc, 0:T], ALU.mult, _mg(t, satok)))
                    t4 = scr_touch(tt(yT[:, fc, 0:T], gu[:, 0:T], sz[:, 0:T], ALU.mult, _mg(t3, t2, yw)))
                    guB.wrote(t4)
                    sigB[0].wrote(t4)
                    sigB[1].wrote(t4)
                    ytoks = _mg(ytoks, t4)
                yTB.wrote(ytoks)
                if l == 0 and j == 0:
                    dump(1, yT[:].bitcast(F32).rearrange("p a b -> p (a b)"), 3072, ytoks, [yTB])
                outproj("a", True)
                if l == 0 and j == 0:
                    dump(2, mT[:].bitcast(F32).rearrange("p a b -> p (a b)"), 6144, _mg(*[b_.rd() for b_ in mTB]), mTB)

                scr_phase()
                sp = scr_prev[0]
                yw = yTB.wr()
                ytoks = {}
                bbuf = [Buf(), Buf()]
                for fc in range(4):
                    bs_ = fc % 2
                    base = bs_ * 3584
                    tb0 = scr[:, base:base + TX]
                    yy = scr[:, base + 768:base + 768 + TX + 2]
                    acc = scr[:, base + 1544:base + 1544 + T]
                    szb = scr[:, base + 2312:base + 2312 + T]
                    wdep = _mg(sp, bbuf[bs_].wr())
                    sA, _ = ring_next()
                    p, tk = fm_job(sA, 0, 8, hT, hd, coff=0, Tc=TX)
                    t = scr_touch(act(bv(tb0, TX), psv(p, TX), AF.Copy, _mg(tk, wdep)))
                    ps_rel(p, t)
                    p, tk = fm_job(sA, 1024, 8, hT, hd, coff=0, Tc=TX)
                    ring_rel(sA, tk)
                    tc = scr_touch(cp(yy[:, 0:2], yyc[:, fc, :], _mg(wdep, yycB.rd())))
                    t = scr_touch(tt(bv(yy[:, 2:2 + TX], TX), psv(p, TX), bv(tb0, TX), ALU.mult, _mg(tk, t, wdep)))
                    ps_rel(p, t)
                    t = _mg(t, tc)
                    tcar = cp(yyc[:, fc, :], yy[:, TX:TX + 2], _mg(t, yycB.wr()))
                    yycB.wrote(tcar, fresh=False)
                    t1 = scr_touch(ts(acc, yy[:, 2 + co:2 + co + T], cw[:, fc, 2:3], cb[:, fc:fc + 1], ALU.mult, ALU.add, _mg(t, cld, wdep)))
                    t1 = scr_touch(stt(acc, yy[:, 1 + co:1 + co + T], cw[:, fc, 1:2], acc, ALU.mult, ALU.add, t1))
                    t1 = scr_touch(stt(acc, yy[:, co:co + T], cw[:, fc, 0:1], acc, ALU.mult, ALU.add, t1))
                    sB, _ = ring_next()
                    p, tk = fm_job(sB, 0, 8, hT, hd, coff=co)
                    t1 = scr_touch(tt(bv(acc), psv(p), bv(acc), ALU.mult, _mg(tk, t1)))
                    ps_rel(p, t1)
                    p, tk = fm_job(sB, 1024, 8, hT, hd, coff=co)
                    hTB.read(tk)
                    ring_rel(sB, tk)
                    t2 = scr_touch(act(bv(szb), psv(p), AF.Silu, _mg(tk, wdep)))
                    ps_rel(p, t2)
                    t3 = scr_touch(tt(yT[:, fc, 0:T], acc, szb, ALU.mult, _mg(t1, t2, yw, tcar)))
                    bbuf[bs_].wrote(t3)
                    ytoks = _mg(ytoks, t3)
                yTB.wrote(ytoks)
                if l == 0 and j == 0:
                    dump(3, yT[:].bitcast(F32).rearrange("p a b -> p (a b)"), 3072, ytoks, [yTB])
                outproj("b", False)

                scr_phase()
                sp = scr_prev[0]
                pT = yT
                szc = [scr[:, 0:TM], scr[:, TM:2 * TM]]
                ywC = yTB.wr()
                H = 16
                xa = scr[:, 1536:1536 + H + TM]
                xb_ = scr[:, 2320:2320 + H + TM]
                xraws = [scr[:, 3104:3104 + H + TM], scr[:, 3904:3904 + H + TM]]
                tmp16 = scr[:, 3888:3904]
                xrawB = [Buf(), Buf()]
                END = H + TX
                ptok_g = {}
                yw = yTB.wr()
                ytoks = {}
                szB = [Buf(), Buf()]
                sxs = [ring_next()[0], ring_next()[0]]
                szs = [ring_next()[0], ring_next()[0]]
                zct = {}
                chain = {"t": {}}
                used_c = [0, 0]
                used_z = [0, 0]

                def xc_stage(gi):
                    w = (2, 4, 8, 16)[gi]
                    xraw = xraws[gi % 2]
                    sx = sxs[gi // 2]
                    p, tk = fm_job(sx, (gi % 2) * 1024, 8, hT, hd, coff=0, Tc=TX)
                    hTB.read(tk)
                    used_c[gi // 2] += 1
                    if used_c[gi // 2] == 2:
                        ring_rel(sx, tk)
                    tcy = scr_touch(cp(xraw[:, 0:H], xcc[:, gi, :], _mg(sp, xrawB[gi % 2].wr(), xccB.rd())))
                    t = scr_touch(act(bv(xraw[:, H:END], TX), psv(p, TX), AF.Copy, _mg(tk, sp, xrawB[gi % 2].wr())))
                    ps_rel(p, t)
                    t = _mg(t, tcy)
                    tcar = cp(xcc[:, gi, :], xraw[:, TX:TX + H], _mg(t, xccB.wr()))
                    xccB.wrote(tcar, fresh=False)
                    src = xraw
                    cur = _mg(t, tcar, chain["t"])
                    sh = 1
                    k_ = 0
                    while sh < w:
                        dst = (xa, xb_)[k_ % 2]
                        lo = 2 * sh - 1
                        cur = scr_touch(tt(dst[:, lo:END], src[:, lo:END], src[:, lo - sh:END - sh], ALU.add, _mg(cur, sp)))
                        src = dst
                        sh *= 2
                        k_ += 1
                    c0 = H + co
                    t2 = scr_touch(stt(pT[:, gi, 0:T], src[:, c0:c0 + T], 1.0 / w, xraw[:, c0:c0 + T], ALU.mult, ALU.subtract,
                                       _mg(cur, ywC)))
                    if t0 == 0:
                        cs = H + 128
                        pc = (1 - sk) * 128
                        t3 = scr_touch(tt(tmp16, src[:, cs:cs + H], invc[:, gi, :], ALU.mult, _mg(t2, cst_g)))
                        t2 = scr_touch(tt(pT[:, gi, pc:pc + H], tmp16, xraw[:, cs:cs + H], ALU.subtract, t3))
                    xrawB[gi % 2].wrote(t2)
                    chain["t"] = t2
                    ptok_g[gi] = t2

                def zc_stage(gi):
                    sz_s = szs[gi // 2]
                    p, tk = fm_job(sz_s, (gi % 2) * 1024, 8, hT, hd, coff=co)
                    hTB.read(tk)
                    used_z[gi // 2] += 1
                    if used_z[gi // 2] == 2:
                        ring_rel(sz_s, tk)
                    t = scr_touch(act(bv(szc[gi % 2][:, 0:T]), psv(p), AF.Silu, _mg(tk, sp, szB[gi % 2].wr())))
                    ps_rel(p, t)
                    zct[gi] = t

                def wp_stage(gi):
                    p, pd = ps_alloc()
                    tk = None
                    for ci, (c0, c1) in enumerate([(0, cwid), (cwid, T)]):
                        tk = mm(ps[:, 2 * p + ci, 0:cwid], wpool[:, gi * 128:(gi + 1) * 128], pT[:, gi, c0:c1], True, True,
                                _mg(pd, ptok_g[gi], cst_g), ms=True)
                    scr_touch(tk)
                    t2 = scr_touch(stt(bv(yT[:, gi, 0:T]), psv(p), psc[:, gi:gi + 1], bv(szc[gi % 2][:, 0:T]), ALU.mult, ALU.mult,
                                       _mg(tk, zct[gi], yw, cld)))
                    ps_rel(p, t2)
                    szB[gi % 2].wrote(t2)
                    return t2

                xc_stage(3)
                xc_stage(2)
                for gi in (3, 2, 1, 0):
                    zc_stage(gi)
                    ytoks = _mg(ytoks, wp_stage(gi))
                    if gi - 2 >= 0:
                        xc_stage(gi - 2)
                yTB.wrote(ytoks)
                if l == 0 and j == 0:
                    dump(4, yT[:].bitcast(F32).rearrange("p a b -> p (a b)"), 3072, ytoks, [yTB])
                nxt = passes[n + 1] if n + 1 < len(passes) else None
                need_setup = nxt is not None and nxt[0] != l
                if need_setup:
                    setup_A(nxt[0], all_now())
                outproj("c", False)
                if need_setup:
                    setup_B(nxt[0])
                if l == 0 and j == 0:
                    dump(5, mT[:].bitcast(F32).rearrange("p a b -> p (a b)"), 6144, _mg(*[b_.rd() for b_ in mTB]), mTB)

                scr_phase()
                sp = scr_prev[0]
                xo = v3(scr[:, 0:NTM * 1024], 1024)
                fgb = scr[:, 6144:7168]
                md = _mg(*[b_.rd() for b_ in mTB])
                p0_list = []
                if nxt is not None:
                    nNT = SUPER[nxt[1]]
                    nt0 = sum(SUPER[:nxt[1]])
                    p0_list = [(nxt[0], nt0 + i, i) for i in range(nNT)]
                hdeps_n = hTB.wr()
                hw = {}
                stA = phase0_A(*p0_list.pop(0)) if p0_list else None
                if l == DEPTH - 1:
                    tfg = scr_touch(dma("pool", fgb, fg_d, "fgl", sp))
                xo_tok = {}
                x1_tok = {}
                xo_reg = [[{}, {}] for _ in range(NTM)]
                ctx["sp_C"] = sp
                fg_rd = dict(tfg) if l == DEPTH - 1 else {}
                add_tok = {}
                ctx["fg_rd"] = fg_rd
                pend = []

                def fin_a(i, dep):
                    junk = sig[:].rearrange("p a b -> p (a b)")[:, 0:D]
                    t = scr_touch(act(junk, xo[:, i, :], AF.Square, _mg(dep, sigB[0].wr(), sigB[1].wr(), stat2B[i].wr()),
                                      accum=ssq2[:, i:i + 1]))
                    sigB[0].w = _mg(sigB[0].w, t)
                    sigB[1].w = _mg(sigB[1].w, t)
                    t = rsqrt(rst2[:, i:i + 1], ssq2[:, i:i + 1], tmpa2[:, i:i + 1], tmpb2[:, i:i + 1], 1.0 / D, RMS_EPS, t)
                    pend.append((i, t))

                def fin_b(i, t):
                    t = scr_touch(stt(xo[:, i, :], xo[:, i, :], rst2[:, i:i + 1], fgb, ALU.mult, ALU.mult, _mg(t, tfg)))
                    stat2B[i].wrote(t)
                    ctx["fg_rd"] = _mg(ctx["fg_rd"], t)
                    if t0 + i >= 1:
                        r0 = (t0 + i - 1) * 128
                        t = scr_touch(dma("sp", out_d[r0:r0 + 128, :], xo[:, i, :], "so%d" % i, t))
                        ctx["out_toks"] = _mg(ctx.get("out_toks"), t)
                    xo_reg[i][0] = xo_reg[i][1] = t

                for half in range(2):
                    sa, _ = ring_next()
                    sb2, _ = ring_next()
                    src = xsrc[(t0 + sk) * 128:(t0 + NT) * 128, half * 512:(half + 1) * 512].rearrange("(i p) d -> p i d", p=128)
                    tx = scr_touch(dma("pool", xo[:, sk:NT, half * 512:(half + 1) * 512], src, "xin%d" % half,
                                       _mg(sp, x1_st.get((l - 1, j)) if l > 0 else None)))
                    su = {}
                    for i0 in range(sk, NT, 2):
                        ni = min(2, NT - i0)
                        p, pd = ps_alloc()
                        tk = None
                        for ii in range(ni):
                            i = i0 + ii
                            for h, s in enumerate((sa, sb2)):
                                for k in range(8):
                                    tk = mm(ps[:, 2 * p + ii, h * 256:(h + 1) * 256], mT[:, k, (i - sk) * 128:(i - sk + 1) * 128],
                                            ring[:, s, k * 256:(k + 1) * 256], k == 0, k == 7,
                                            _mg(pd, md, slot_tok[sa], slot_tok[sb2]) if k == 0 else None, ms=(k == 7))
                        su = _mg(su, tk)
                        do_b = stA is not None
                        if do_b:
                            nstA = phase0_A(*p0_list.pop(0)) if p0_list else None
                        t = scr_touch(tt(xo[:, i0:i0 + ni, half * 512:(half + 1) * 512], ps[:, 2 * p:2 * p + ni, :],
                                         xo[:, i0:i0 + ni, half * 512:(half + 1) * 512], ALU.add, _mg(tk, tx, sp)))
                        ps_rel(p, t)
                        xo_tok = _mg(xo_tok, t)
                        add_tok[(half, i0)] = t
                        if do_b:
                            hw = _mg(hw, phase0_B(stA, hdeps_n))
                            stA = nstA
                        if l == DEPTH - 1 and half == 1:
                            while len(pend) > 0:
                                fin_b(*pend.pop(0))
                            for i_ in range(i0, i0 + ni):
                                fin_a(i_, _mg(t, add_tok[(0, i0)]))
                        if l < DEPTH - 1:
                            dst = x1_d[(t0 + i0) * 128:(t0 + i0 + ni) * 128, half * 512:(half + 1) * 512].rearrange(
                                "(i p) d -> p i d", p=128)
                            t_ = scr_touch(dma("sp", dst, xo[:, i0:i0 + ni, half * 512:(half + 1) * 512], "sx%d" % (half * 3 + i0 // 2), t))
                            x1_tok = _mg(x1_tok, t_)
                            for i_ in range(i0, i0 + ni):
                                xo_reg[i_][half] = t_
                    for b_ in mTB:
                        b_.read(su)
                    ring_rel(sa, su)
                    ring_rel(sb2, su)
                while stA is not None:
                    nstA = phase0_A(*p0_list.pop(0)) if p0_list else None
                    hw = _mg(hw, phase0_B(stA, hdeps_n))
                    stA = nstA
                if nxt is not None:
                    hTB.wrote(hw)
                if l == 0 and j == 0:
                    dump(6, scr[:, 0:6144], 6144, xo_tok)
                ctx["xo_reg"] = xo_reg
                if l < DEPTH - 1:
                    x1_st[(l, j)] = x1_tok
                else:
                    while pend:
                        fin_b(*pend.pop(0))


        S.op("pool", lambda e: e.memset(tmp8[:, 0:1], 0.0), _mg(ctx.get("out_toks"), dbg_toks), ms=False)

        with nc.Block() as block:
            @block.sync
            def _(e):
                S.run("sp", e, sems)

            @block.tensor
            def _(e):
                S.run("pe", e, sems)

            @block.scalar
            def _(e):
                S.run("act", e, sems)

            @block.vector
            def _(e):
                S.run("dve", e, sems)

            @block.gpsimd
            def _(e):
                S.run("pool", e, sems)
    return nc


def _fm_unit(W, c0):
    K = W.shape[0]
    return W[:, c0:c0 + 128].reshape(K // 128, 128, 128).transpose(1, 0, 2).reshape(128, -1)


def _tm_unit(W, c0):
    K = W.shape[0]
    return W[:, c0:c0 + 256].reshape(K // 128, 128, 256).transpose(1, 0, 2).reshape(128, -1)


def _weight_stream(w_in, w_pa, w_pb, w_pc, w_o):
    slots = []

    def outproj(Wp, cg0):
        def g(i):
            return np.concatenate([_fm_unit(w_in, cg0 + (2 * i) * 128), _fm_unit(w_in, cg0 + (2 * i + 1) * 128)], axis=1)

        def wp(i):
            return np.concatenate([_fm_unit(Wp, (4 * i + q) * 128) for q in range(4)], axis=1)
        return [g(0), wp(0), g(1), g(2), wp(1), g(3)]

    slots += [_tm_unit(w_in, C_V), _tm_unit(w_in, C_V + 256)]
    for fc in range(4):
        slots.append(np.concatenate([_fm_unit(w_in, C_U + fc * 128), _fm_unit(w_in, C_ZA + fc * 128)], axis=1))
    slots += outproj(w_pa, C_GA)
    for fc in range(4):
        slots.append(np.concatenate([_fm_unit(w_in, C_XB + fc * 128), _fm_unit(w_in, C_CG + fc * 128)], axis=1))
        slots.append(np.concatenate([_fm_unit(w_in, C_BG + fc * 128), _fm_unit(w_in, C_ZB + fc * 128)], axis=1))
    slots += outproj(w_pb, C_GB)
    slots.append(np.concatenate([_fm_unit(w_in, C_XC), _fm_unit(w_in, C_XC + 128)], axis=1))
    slots.append(np.concatenate([_fm_unit(w_in, C_XC + 256), _fm_unit(w_in, C_XC + 384)], axis=1))
    slots.append(np.concatenate([_fm_unit(w_in, C_ZC), _fm_unit(w_in, C_ZC + 128)], axis=1))
    slots.append(np.concatenate([_fm_unit(w_in, C_ZC + 256), _fm_unit(w_in, C_ZC + 384)], axis=1))
    slots += outproj(w_pc, C_GC)
    for cb in range(4):
        slots.append(_tm_unit(w_o, cb * 256))
    assert len(slots) == NSLOT_L
    return np.stack(slots, axis=0)


def _band_consts():
    s = np.arange(128)[:, None]
    t = np.arange(128)[None, :]
    band = np.zeros((128, 4, 128), np.float32)
    bandp = np.zeros((128, 4, 128), np.float32)
    bandf = np.zeros((128, 4, 128), np.float32)
    for gi, w in enumerate((2, 4, 8, 16)):
        inw = (s <= t) & (s > t - w)
        band[:, gi, :] = np.where(inw, np.float32(1.0 / w), 0.0) - np.where(s == t, 1.0, 0.0)
        inwp = ((s - 128) > t - w)
        bandp[:, gi, :] = np.where(inwp, np.float32(1.0 / w), 0.0)
        cnt = np.minimum(t + 1, w).astype(np.float32)
        bandf[:, gi, :] = np.where(inw, np.float32(1.0) / cnt, 0.0) - np.where(s == t, 1.0, 0.0)
    return band.astype(np.float32), bandp.astype(np.float32), bandf.astype(np.float32)


_NC_CACHE = {}


def kernel(x, norm_g, w_in, ln_g, ln_b, w_s, b_s, conv_w, conv_b, w_pool, pool_scale,
           w_pa, w_pb, w_pc, w_o, final_g):
    f = lambda a: np.ascontiguousarray(np.asarray(a, dtype=np.float32))
    x, norm_g, w_in, ln_g, ln_b, w_s, b_s = map(f, (x, norm_g, w_in, ln_g, ln_b, w_s, b_s))
    conv_w, conv_b, w_pool, pool_scale = map(f, (conv_w, conv_b, w_pool, pool_scale))
    w_pa, w_pb, w_pc, w_o, final_g = map(f, (w_pa, w_pb, w_pc, w_o, final_g))
    B, SEQ, _ = x.shape
    cores_per_b = NCORES // B

    ws = np.concatenate([_weight_stream(w_in[l], w_pa[l], w_pb[l], w_pc[l], w_o[l]) for l in range(DEPTH)], axis=0)
    ws = np.ascontiguousarray(ws, dtype=np.float32)

    cl = np.zeros((DEPTH, 128, CL), np.float32)
    for l in range(DEPTH):
        cl[l, :, CL_G:CL_G + 8] = norm_g[l].reshape(8, 128).T
        cl[l, :, CL_LNB:CL_LNB + 4] = ln_b[l].reshape(4, 128).T
        cl[l, :, CL_CW:CL_CW + 12] = conv_w[l].reshape(3, 4, 128).transpose(2, 1, 0).reshape(128, 12)
        cl[l, :, CL_CB:CL_CB + 4] = conv_b[l].reshape(4, 128).T
        cl[l, :, CL_PS:CL_PS + 4] = pool_scale[l].reshape(4, 128).T
        cl[l, :, CL_LNG:CL_LNG + 4] = ln_g[l].reshape(4, 128).T
        cl[l, :, CL_LNGB:CL_LNGB + 512] = np.broadcast_to(ln_g[l][None, :], (128, 512))
        bs = b_s[l].reshape(4, 2, 1, 128)
        cl[l, :, CL_BS:CL_BS + 512] = np.broadcast_to(bs, (4, 2, 64, 128)).transpose(1, 2, 0, 3).reshape(128, 512)
        cl[l, :, CL_WS:CL_WS + 1024] = w_s[l].transpose(2, 0, 1).reshape(128, 1024)
    band, bandp, bandf = _band_consts()
    ones = np.ones((128, 128), np.float32)
    def cg_for(first):
        inv = np.zeros((4, 16), np.float32)
        for gi_, w_ in enumerate((2, 4, 8, 16)):
            cnt_ = np.minimum(np.arange(1, 17), w_) if first else np.full(16, w_)
            inv[gi_] = (np.float32(1.0) / cnt_.astype(np.float32)).astype(np.float32)
        return np.ascontiguousarray(np.concatenate(
            [np.eye(128, dtype=np.float32), np.triu(np.ones((128, 128), np.float32)),
             np.broadcast_to(inv.reshape(1, 64), (128, 64))], axis=1), dtype=np.float32)

    cg_first, cg_other = cg_for(True), cg_for(False)
    fg = np.ascontiguousarray(np.broadcast_to(final_g[None, :], (128, D)), dtype=np.float32)

    def cr_for(first):
        parts = [w_pool[l].transpose(1, 0, 2).reshape(128, 512) for l in range(DEPTH)]
        parts += [band.reshape(128, 512), bandp.reshape(128, 512),
                  (bandf if first else band).reshape(128, 512), ones]
        return np.ascontiguousarray(np.concatenate(parts, axis=1), dtype=np.float32)

    cr_first, cr_other = cr_for(True), cr_for(False)

    in_maps = []
    for c in range(NCORES):
        b = c // cores_per_b
        s0 = (c % cores_per_b) * TOK_CORE
        xc = np.zeros((NTILES * 128, D), np.float32)
        if s0 == 0:
            xc[128:] = x[b, 0:TOK_CORE]
        else:
            xc[:] = x[b, s0 - 128:s0 + TOK_CORE]
        in_maps.append({"x": xc, "ws": ws, "cl": cl, "cr": cr_first if s0 == 0 else cr_other, "cg": cg_first if s0 == 0 else cg_other, "fg": fg})

    if "nc" not in _NC_CACHE:
        _NC_CACHE["nc"] = build_program()
    nc = _NC_CACHE["nc"]
    res = run_bass_kernel_spmd(nc, in_maps, core_ids=list(range(NCORES)))
    if _DBG:
        _NC_CACHE["dbg"] = [res.results[c]["dbg"] for c in range(NCORES)]
    out = np.empty((B, SEQ, D), np.float32)
    for c in range(NCORES):
        b = c // cores_per_b
        s0 = (c % cores_per_b) * TOK_CORE
        out[b, s0:s0 + TOK_CORE] = res.results[c]["out"]
    return out
```
